# Optimizing a Trainium2 kernel written in Bass

```python
import jax, jax.numpy as jnp
from jax import lax
import numpy as np

D_MODEL = 1024
BATCH = 8
SEQ = 2048
DEPTH = 1

F32 = jnp.float32
EPS = 1e-6
MEM_LEN = 256
GRID_W = 64
CONV_W = 3
ML_HEADS = 4
ML_HEAD_DIM = 128
ML_WIDTH = ML_HEADS * ML_HEAD_DIM
ML_CHUNK = 64
NA_HEADS = 8
NA_HEAD_DIM = 64
NA_WIDTH = NA_HEADS * NA_HEAD_DIM
NA_WIN_ROWS_MAX = 8
NA_WIN_COLS = 16
XA_HEADS = 4
XA_HEAD_DIM = 128
XA_WIDTH = XA_HEADS * XA_HEAD_DIM
N_BRANCHES = 3
D_FF = 2816
IN_SPLITS = (ML_WIDTH, ML_WIDTH, ML_WIDTH, ML_WIDTH, 4 * ML_HEADS, 3 * NA_WIDTH, XA_WIDTH, N_BRANCHES * D_MODEL)
IN_WIDTH = sum(IN_SPLITS)
NEG_INIT = -1e30

kernel_name = "hybrid_mlstm_natten_memxattn_block"


def rms_norm(x, g):
    xf = x.astype(F32)
    y = xf * lax.rsqrt(jnp.mean(xf * xf, axis=-1, keepdims=True) + EPS)
    return (y * g.astype(F32)).astype(x.dtype)


def dwconv3(x, w, b):
    xp = jnp.pad(x, ((0, 0), (1, 1), (0, 0)))
    return xp[:, :-2] * w[0] + xp[:, 1:-1] * w[1] + xp[:, 2:] * w[2] + b


def mlstm_chunkwise(q, k, v, i_pre, f_pre):
    B, H, S, Dh = q.shape
    L = ML_CHUNK
    NC = S // L
    q = q.astype(F32).reshape(B, H, NC, L, Dh) * (Dh ** -0.5)
    k = k.astype(F32).reshape(B, H, NC, L, Dh)
    v = v.astype(F32).reshape(B, H, NC, L, Dh)
    log_f = jax.nn.log_sigmoid(f_pre.astype(F32)).reshape(B, H, NC, L)
    log_i = i_pre.astype(F32).reshape(B, H, NC, L)
    a = jnp.cumsum(log_f, axis=-1)
    g = a[..., -1]
    w_end = g[..., None] - a + log_i
    m_end = jnp.max(w_end, axis=-1)

    def step(carry, xs):
        C, n, m = carry
        k_c, v_c, w_c, mw_c, g_c = xs
        m_new = jnp.maximum(g_c + m, mw_c)
        decay = jnp.exp(g_c + m - m_new)
        wt = jnp.exp(w_c - m_new[..., None])
        C_new = decay[..., None, None] * C + jnp.einsum('bhl,bhld,bhle->bhde', wt, v_c, k_c)
        n_new = decay[..., None] * n + jnp.einsum('bhl,bhle->bhe', wt, k_c)
        return (C_new, n_new, m_new), (C, n, m)

    init = (jnp.zeros((B, H, Dh, Dh), F32), jnp.zeros((B, H, Dh), F32), jnp.full((B, H), NEG_INIT, F32))
    xs = (jnp.moveaxis(k, 2, 0), jnp.moveaxis(v, 2, 0), jnp.moveaxis(w_end, 2, 0),
          jnp.moveaxis(m_end, 2, 0), jnp.moveaxis(g, 2, 0))
    _, (C_prev, n_prev, m_prev) = lax.scan(step, init, xs)
    C_prev = jnp.moveaxis(C_prev, 0, 2)
    n_prev = jnp.moveaxis(n_prev, 0, 2)
    m_prev = jnp.moveaxis(m_prev, 0, 2)

    causal = jnp.tril(jnp.ones((L, L), dtype=bool))
    D = a[..., :, None] - a[..., None, :] + log_i[..., None, :]
    D = jnp.where(causal, D, -jnp.inf)
    inter_log = a + m_prev[..., None]
    m_j = jnp.maximum(inter_log, jnp.max(D, axis=-1))
    qk = jnp.einsum('bhcjd,bhcsd->bhcjs', q, k)
    P = jnp.exp(D - m_j[..., None]) * qk
    inter_w = jnp.exp(inter_log - m_j)
    num = jnp.einsum('bhcjs,bhcsd->bhcjd', P, v) + inter_w[..., None] * jnp.einsum('bhcde,bhcje->bhcjd', C_prev, q)
    den = jnp.sum(P, axis=-1) + inter_w * jnp.einsum('bhce,bhcje->bhcj', n_prev, q)
    h = num / jnp.maximum(jnp.abs(den), jnp.exp(-m_j))[..., None]
    return h.reshape(B, H, S, Dh)


def mlstm_branch(q_pre, k_pre, v, o_pre, gates, w_conv, b_conv, b_ig, b_fg, norm_g):
    B, S, _ = q_pre.shape
    qk = jax.nn.silu(dwconv3(jnp.concatenate([q_pre, k_pre], axis=-1), w_conv, b_conv))
    q, k = jnp.split(qk, 2, axis=-1)
    heads = lambda t: t.reshape(B, S, ML_HEADS, ML_HEAD_DIM).transpose(0, 2, 1, 3)
    qh, kh, vh = heads(q), heads(k), heads(v)
    gt = gates.astype(F32).reshape(B, S, 4, ML_HEADS).transpose(2, 0, 3, 1)
    b_ig = b_ig.astype(F32)
    b_fg = b_fg.astype(F32)
    i_fw = gt[0] + b_ig[0][None, :, None]
    f_fw = gt[1] + b_fg[0][None, :, None]
    i_bw = gt[2] + b_ig[1][None, :, None]
    f_bw = gt[3] + b_fg[1][None, :, None]
    h_fw = mlstm_chunkwise(qh, kh, vh, i_fw, f_fw)
    flip = lambda t: jnp.flip(t, axis=2)
    h_bw = flip(mlstm_chunkwise(flip(qh), flip(kh), flip(vh), flip(i_bw), flip(f_bw)))
    h = h_fw + h_bw
    h = h * lax.rsqrt(jnp.mean(h * h, axis=-1, keepdims=True) + EPS)
    h = h * norm_g.astype(F32).reshape(ML_HEADS, 1, ML_HEAD_DIM)
    h = h.transpose(0, 2, 1, 3).reshape(B, S, ML_WIDTH)
    return (jax.nn.sigmoid(o_pre.astype(F32)) * h).astype(q_pre.dtype)


def neighbourhood_attention(q, k, v, rpb):
    B, S, _ = q.shape
    rows = S // GRID_W
    kr = min(NA_WIN_ROWS_MAX, rows)
    grid = lambda t: t.reshape(B, rows, GRID_W, NA_HEADS, NA_HEAD_DIM).transpose(0, 3, 1, 2, 4)
    qg, kg, vg = grid(q).astype(F32), grid(k), grid(v)
    r = jnp.arange(rows)
    row_start = jnp.clip(r - kr // 2, 0, rows - kr)
    row_idx = row_start[:, None] + jnp.arange(kr)[None, :]
    k_band = kg[:, :, row_idx].astype(F32)
    v_band = vg[:, :, row_idx].astype(F32)
    c = jnp.arange(GRID_W)
    col_start = jnp.clip(c - NA_WIN_COLS // 2, 0, GRID_W - NA_WIN_COLS)
    col_ok = (c[None, :] >= col_start[:, None]) & (c[None, :] < col_start[:, None] + NA_WIN_COLS)
    dr = row_idx - r[:, None] + (NA_WIN_ROWS_MAX - 1)
    dc = jnp.clip(c[None, :] - c[:, None] + (NA_WIN_COLS - 1), 0, 2 * NA_WIN_COLS - 2)
    bias = rpb.astype(F32)[:, dr[:, None, :, None], dc[None, :, None, :]]
    s = jnp.einsum('bhrcd,bhrkjd->bhrckj', qg, k_band) * (NA_HEAD_DIM ** -0.5) + bias[None]
    s = jnp.where(col_ok[None, None, None, :, None, :], s, -jnp.inf)
    p = jax.nn.softmax(s.reshape(B, NA_HEADS, rows, GRID_W, kr * GRID_W), axis=-1).reshape(s.shape)
    o = jnp.einsum('bhrckj,bhrkjd->bhrcd', p, v_band)
    return o.transpose(0, 2, 3, 1, 4).reshape(B, S, NA_WIDTH).astype(q.dtype)


def memory_cross_attention(q, mem_k, mem_v):
    B, S, _ = q.shape
    M = mem_k.shape[1]
    qh = q.reshape(B, S, XA_HEADS, XA_HEAD_DIM).astype(F32)
    kh = mem_k.reshape(B, M, XA_HEADS, XA_HEAD_DIM).astype(F32)
    vh = mem_v.reshape(B, M, XA_HEADS, XA_HEAD_DIM).astype(F32)
    s = jnp.einsum('bshd,bmhd->bhsm', qh, kh) * (XA_HEAD_DIM ** -0.5)
    p = jax.nn.softmax(s, axis=-1)
    o = jnp.einsum('bhsm,bmhd->bshd', p, vh)
    return o.reshape(B, S, XA_WIDTH).astype(q.dtype)


def conv_glu_ffn(h, w_up, w_conv, b_conv, w_down):
    a, u = jnp.split(h @ w_up, 2, axis=-1)
    return (jax.nn.gelu(dwconv3(a, w_conv, b_conv)) * u) @ w_down


def setup_inputs(seed: int = 0) -> dict:
    key = jax.random.key(seed)
    ks = jax.random.split(key, 24)

    def dense(k, shape, fan_in):
        return jax.random.normal(k, shape, F32) * (fan_in ** -0.5)

    def gain(k, shape):
        return 1.0 + 0.05 * jax.random.normal(k, shape, F32)

    def small(k, shape, scale):
        return scale * jax.random.normal(k, shape, F32)

    L = DEPTH
    return {
        "x": jax.random.normal(ks[0], (BATCH, SEQ, D_MODEL), F32),
        "mem": jax.random.normal(ks[1], (BATCH, MEM_LEN, D_MODEL), F32),
        "mix_norm_g": gain(ks[2], (L, D_MODEL)),
        "w_in": dense(ks[3], (L, D_MODEL, IN_WIDTH), D_MODEL),
        "b_ml_igate": small(ks[4], (L, 2, ML_HEADS), 0.1),
        "b_ml_fgate": jnp.linspace(3.0, 6.0, ML_HEADS, dtype=F32) + small(ks[5], (L, 2, ML_HEADS), 0.1),
        "w_ml_conv": dense(ks[6], (L, CONV_W, 2 * ML_WIDTH), CONV_W),
        "b_ml_conv": small(ks[7], (L, 2 * ML_WIDTH), 0.02),
        "ml_norm_g": gain(ks[8], (L, ML_WIDTH)),
        "na_rpb": small(ks[9], (L, NA_HEADS, 2 * NA_WIN_ROWS_MAX - 1, 2 * NA_WIN_COLS - 1), 0.1),
        "mem_norm_g": gain(ks[10], (L, D_MODEL)),
        "w_mem_kv": dense(ks[11], (L, D_MODEL, 2 * XA_WIDTH), D_MODEL),
        "b_merge_gate": small(ks[12], (L, N_BRANCHES * D_MODEL), 0.02),
        "w_br_ml": dense(ks[13], (L, ML_WIDTH, D_MODEL), ML_WIDTH),
        "w_br_na": dense(ks[14], (L, NA_WIDTH, D_MODEL), NA_WIDTH),
        "w_br_xa": dense(ks[15], (L, XA_WIDTH, D_MODEL), XA_WIDTH),
        "w_out": dense(ks[16], (L, D_MODEL, D_MODEL), D_MODEL),
        "ffn_norm_g": gain(ks[17], (L, D_MODEL)),
        "w_ffn_up": dense(ks[18], (L, D_MODEL, 2 * D_FF), D_MODEL),
        "w_ffn_conv": dense(ks[19], (L, CONV_W, D_FF), CONV_W),
        "b_ffn_conv": small(ks[20], (L, D_FF), 0.02),
        "w_ffn_down": dense(ks[21], (L, D_FF, D_MODEL), D_FF),
        "final_norm_g": gain(ks[22], (D_MODEL,)),
    }


def reference(x, mem, mix_norm_g, w_in, b_ml_igate, b_ml_fgate, w_ml_conv, b_ml_conv, ml_norm_g,
              na_rpb, mem_norm_g, w_mem_kv, b_merge_gate, w_br_ml, w_br_na, w_br_xa, w_out,
              ffn_norm_g, w_ffn_up, w_ffn_conv, b_ffn_conv, w_ffn_down, final_norm_g):
    split_points = [int(p) for p in np.cumsum(IN_SPLITS)[:-1]]
    for l in range(DEPTH):
        h = rms_norm(x, mix_norm_g[l])
        proj = h @ w_in[l]
        ml_q, ml_k, ml_v, ml_o, ml_gates, na_qkv, xa_q, merge_pre = jnp.split(proj, split_points, axis=-1)
        y_ml = mlstm_branch(ml_q, ml_k, ml_v, ml_o, ml_gates, w_ml_conv[l], b_ml_conv[l],
                            b_ml_igate[l], b_ml_fgate[l], ml_norm_g[l])
        na_q, na_k, na_v = jnp.split(na_qkv, 3, axis=-1)
        y_na = neighbourhood_attention(na_q, na_k, na_v, na_rpb[l])
        mem_k, mem_v = jnp.split(rms_norm(mem, mem_norm_g[l]) @ w_mem_kv[l], 2, axis=-1)
        y_xa = memory_cross_attention(xa_q, mem_k, mem_v)
        gates = jax.nn.sigmoid((merge_pre + b_merge_gate[l]).astype(F32)).astype(x.dtype)
        g_ml, g_na, g_xa = jnp.split(gates, N_BRANCHES, axis=-1)
        merged = g_ml * (y_ml @ w_br_ml[l]) + g_na * (y_na @ w_br_na[l]) + g_xa * (y_xa @ w_br_xa[l])
        x = x + merged @ w_out[l]
        x = x + conv_glu_ffn(rms_norm(x, ffn_norm_g[l]), w_ffn_up[l], w_ffn_conv[l], b_ffn_conv[l], w_ffn_down[l])
    return rms_norm(x, final_norm_g)
```

```python
import math
import numpy as np
import concourse.bass as bass
import concourse.mybir as mybir
from concourse.bass_utils import run_bass_kernel_spmd

F32 = mybir.dt.float32
BF16 = mybir.dt.bfloat16
AF = mybir.ActivationFunctionType
ALU = mybir.AluOpType

S = 2048
NT = 16
NPV = 172
NDMA = 8
ENGS = ("pe", "act", "dve", "pool", "sp")
C0 = -0.5 * math.log(128.0)
GELU_K = 1.5957691216057308


class Op:
    __slots__ = ("eng", "fn", "deps", "waited", "semval", "dma", "slot", "dval")


class Sched:
    def __init__(self):
        self.ops = {e: [] for e in ENGS}
        self.lastw = {}
        self.rd = {}
        self.dma_rr = {e: 0 for e in ENGS}
        self.dma_last = {}
        self.pending = {e: [] for e in ENGS}
        self.dmas_since = []

    def add(self, eng, fn, r=(), w=(), dma=False):
        o = Op()
        o.eng = eng; o.fn = fn; o.dma = dma; o.waited = False; o.deps = []
        o.semval = 0; o.slot = 0; o.dval = 0
        deps = []
        for k in r:
            lw = self.lastw.get(k)
            if lw is not None:
                deps.append(lw)
        for k in w:
            lw = self.lastw.get(k)
            if lw is not None:
                deps.append(lw)
            deps.extend(self.rd.get(k, {}).values())
        deps.extend(self.pending[eng])
        self.pending[eng] = []
        if dma:
            o.slot = self.dma_rr[eng] % NDMA
            self.dma_rr[eng] += 1
            prev = self.dma_last.get((eng, o.slot))
            if prev is not None:
                deps.append(prev)
            self.dma_last[(eng, o.slot)] = o
            self.dmas_since.append(o)
        seen = set()
        for d in deps:
            if d is o or id(d) in seen:
                continue
            seen.add(id(d))
            if (not d.dma) and d.eng == eng and eng == "pe":
                continue
            d.waited = True
            o.deps.append(d)
        for k in w:
            self.lastw[k] = o
            self.rd[k] = {}
        for k in r:
            self.rd.setdefault(k, {})[("d", id(o)) if dma else eng] = o
        self.ops[eng].append(o)
        return o

    def barrier(self):
        lasts = []
        for e in ENGS:
            for o in reversed(self.ops[e]):
                if not o.dma:
                    lasts.append(o)
                    break
        lasts.extend(self.dmas_since)
        self.dmas_since = []
        for e in ENGS:
            self.pending[e] = list(self.pending[e]) + lasts

    def finalize(self):
        for e in ENGS:
            c = 0
            for o in self.ops[e]:
                if (not o.dma) and o.waited:
                    c += 1
                    o.semval = c
        cnt = {}
        for e in ENGS:
            for o in self.ops[e]:
                if o.dma:
                    key = (e, o.slot)
                    cnt[key] = cnt.get(key, 0) + 16
                    o.dval = cnt[key]

    def emit(self, e, eng, sems, dsems):
        waited = {}
        for o in self.ops[e]:
            for d in o.deps:
                if d.dma:
                    key = ("d", d.eng, d.slot); s = dsems[(d.eng, d.slot)]; v = d.dval
                else:
                    key = ("c", d.eng); s = sems[d.eng]; v = d.semval
                if waited.get(key, 0) >= v:
                    continue
                eng.wait_ge(s, v)
                waited[key] = v
            ins = o.fn(eng)
            if o.dma:
                ins.then_inc(dsems[(e, o.slot)], 16)
            elif o.waited:
                ins.then_inc(sems[e], 1)


def build(debug=False):
    nc = bass.Bass("TRN2", target_bir_lowering=False)

    def dram(name, shape, kind="ExternalInput"):
        return nc.dram_tensor(name, shape, F32, kind=kind).ap()

    x_d = dram("x", [2048, 1024]); mem_d = dram("mem", [256, 1024])
    win_d = dram("w_in", [128, 8, 7184]); wkv_d = dram("w_kv", [128, 8, 1024])
    wbr_d = dram("w_br", [3, 128, 4, 1024]); wout_d = dram("w_out", [128, 8, 1024])
    wup_d = dram("w_up", [128, 8, 5632]); wdn_d = dram("w_dn", [128, 22, 1024])
    pvec_d = dram("pvec", [128, NPV]); fg_d = dram("fg", [128, 1024]); gb_d = dram("gb", [16, 1])
    idf_d = dram("ident", [128, 128]); tri_d = dram("tri", [128, 2, 128])
    sel_d = dram("sel", [16, 2, 8, 128]); cmb_d = dram("cmb", [16, 3, 16])
    nab_d = dram("nab", [128, 8, 9, 128])
    y_d = dram("y", [2048, 1024], kind="ExternalOutput")
    dbg_d = {}
    if debug:
        dbg_d["HT"] = nc.dram_tensor("dbg_HT", [128, 8, 2048], BF16, kind="ExternalOutput").ap()
        dbg_d["YT"] = nc.dram_tensor("dbg_YT", [128, 12, 2048], BF16, kind="ExternalOutput").ap()
        dbg_d["MT"] = nc.dram_tensor("dbg_MT", [128, 8, 2048], BF16, kind="ExternalOutput").ap()
        dbg_d["X1"] = nc.dram_tensor("dbg_X1", [128, 16, 1024], F32, kind="ExternalOutput").ap()

    Sd = Sched()
    from contextlib import ExitStack
    with ExitStack() as es:
        def sb(name, shape, dtype=F32):
            return es.enter_context(nc.sbuf_tensor("sb_" + name, shape, dtype))
        A1 = sb("A1", [128, 16384]); A2 = sb("A2", [128, 20480]); WA = sb("WA", [128, 6272])
        identb = sb("identb", [128, 128], BF16); identf = sb("identf", [128, 128])
        tri = sb("tri", [128, 2, 128]); sel = sb("sel", [16, 2, 8, 128]); cmb = sb("cmb", [16, 3, 16])
        pvec = sb("pvec", [128, NPV]); gb = sb("gb", [16, 1])
        NB = sb("NB", [128, 9216], BF16)
        hnst = sb("hnst", [128, 2, 1024], BF16)
        junk = sb("junk", [128, 1024], BF16)
        st = sb("st", [128, 256])
        PS = es.enter_context(nc.psum_tensor("PS", [128, 8, 512], F32))
        sems = {e: es.enter_context(nc.semaphore("s_" + e)) for e in ENGS}
        dsems = {(e, i): es.enter_context(nc.semaphore("d_%s%d" % (e, i))) for e in ("sp", "pool") for i in range(NDMA)}
        block = es.enter_context(nc.Block())

        A1b = A1[:, :].bitcast(BF16)
        A2b = A2[:, :].bitcast(BF16)
        WAb = WA[:, :].bitcast(BF16)
        HT = A2b[:, 0:16384].rearrange("p (a b) -> p a b", b=2048)
        YT = A2b[:, 16384:40960].rearrange("p (a b) -> p a b", b=2048)
        X = A1[:, :].rearrange("p (a b) -> p a b", b=1024)
        X1 = A2[:, 0:16384].rearrange("p (a b) -> p a b", b=1024)

        def psf(b0, nb):
            return PS[:, b0:b0 + nb, :].rearrange("p a b -> p (a b)")

        def psb16(b):
            return PS[:, b, :].bitcast(BF16)

        def PSK(*banks):
            return [("ps", b) for b in banks]

        def dma(q, out, in_, r=(), w=()):
            return Sd.add(q, lambda e, out=out, in_=in_: e.dma_start(out=out, in_=in_), r, w, dma=True)

        def pe_mms(lst, r, w, skip=False):
            def fn(e, lst=lst, skip=skip):
                ins = None
                for (o_, l_, r_, st_, sp_) in lst:
                    if skip:
                        ins = e.matmul(o_, l_, r_, start=st_, stop=sp_, skip_group_check=True)
                    else:
                        ins = e.matmul(o_, l_, r_, start=st_, stop=sp_)
                return ins
            return Sd.add("pe", fn, r, w)

        def pe_tr(lst, r, w):
            def fn(e, lst=lst):
                ins = None
                for (o_, i_) in lst:
                    ins = e.transpose(o_, i_, identb[:, :])
                return ins
            return Sd.add("pe", fn, r, w)

        def act(out, in_, func, r, w, bias=None, scale=None, accum=None):
            kw = {}
            if bias is not None: kw["bias"] = bias
            if scale is not None: kw["scale"] = scale
            if accum is not None: kw["accum_out"] = accum
            return Sd.add("act", lambda e, out=out, in_=in_, func=func, kw=kw: e.activation(out=out, in_=in_, func=func, **kw), r, w)

        def ts(eng, out, in0, s1, s2, op0, op1, r, w):
            if op1 is None:
                return Sd.add(eng, lambda e: e.tensor_scalar(out, in0, s1, None, op0), r, w)
            return Sd.add(eng, lambda e: e.tensor_scalar(out, in0, s1, s2, op0, op1), r, w)

        def stt(eng, out, in0, sc, in1, op0, op1, r, w):
            return Sd.add(eng, lambda e: e.scalar_tensor_tensor(out, in0, sc, in1, op0, op1), r, w)

        def tt(eng, out, in0, in1, op, r, w):
            return Sd.add(eng, lambda e: e.tensor_tensor(out, in0, in1, op), r, w)

        def rstd_ops(ssq, ms, rs, scale, r, w):
            ts("dve", ms, ssq, scale, 1e-6, ALU.mult, ALU.add, r, ["rs_tmp"])
            act(ms, ms, AF.Sqrt, [], ["rs_tmp"])
            Sd.add("dve", lambda e: e.reciprocal(rs, ms), ["rs_tmp"], w)

        dma("sp", pvec[:, :], pvec_d, w=["pvec"])
        dma("sp", identf[:, :], idf_d, w=["identf"])
        dma("sp", tri[:, :, :], tri_d, w=["tri"])
        dma("sp", sel[:, :, :, :], sel_d, w=["sel"])
        dma("sp", cmb[:, :, :], cmb_d, w=["cmb"])
        dma("sp", gb[:, :], gb_d, w=["gb"])
        dma("pool", identb[:, :], idf_d, w=["identb"])
        for t in range(NT):
            dma("sp", X[:, t, :], x_d[t * 128:(t + 1) * 128, :], w=[("X", t)])
        nab = NB[:, :].rearrange("p (a b c) -> p a b c", b=9, c=128)
        for h in range(8):
            dma("pool", nab[:, h, :, :], nab_d[:, h, :, :], r=[("X", NT - 1)], w=[("nab", h)])

        def norm_group(src, nt_list, gcol, runs, dst_off, srckey, dstkey, stat0, g0, slots=None, banks=(6, 7)):
            n = len(nt_list)
            ssq = st[:, stat0:stat0 + n]; ms = st[:, stat0 + 16:stat0 + 16 + n]; rs = st[:, stat0 + 32:stat0 + 32 + n]
            idx = list(range(g0, min(g0 + 4, n)))
            for i in idx:
                t = nt_list[i]
                act(junk[:, :], src[:, t, :], AF.Square, [(srckey, t)], [("ssq", stat0, i), "junk"], accum=ssq[:, i:i + 1])
            gs = slice(idx[0], idx[-1] + 1)
            tk = ("rs_tmp", stat0, g0)
            ts("dve", ms[:, gs], ssq[:, gs], 1.0 / 1024.0, 1e-6, ALU.mult, ALU.add, [("ssq", stat0, i) for i in idx], [tk])
            act(ms[:, gs], ms[:, gs], AF.Sqrt, [], [tk])
            Sd.add("dve", lambda e, gs=gs: e.reciprocal(rs[:, gs], ms[:, gs]), [tk], [("rs", stat0, g0)])
            if slots is None:
                slots = [hnst[:, 0, :], hnst[:, 1, :]]
            nsl = len(slots); nbk = len(banks)

            def scale_tile(i):
                t = nt_list[i]
                sl = i % nsl
                if i % 2 == 0:
                    act(slots[sl], src[:, t, :], AF.Copy, [(srckey, t), ("rs", stat0, g0)], [("hnst", sl)], scale=rs[:, i:i + 1])
                else:
                    ts("dve", slots[sl], src[:, t, :], rs[:, i:i + 1], None, ALU.mult, None,
                       [(srckey, t), ("rs", stat0, g0)], [("hnst", sl)])

            two_phase = nsl >= len(idx)
            if two_phase:
                for i in idx:
                    scale_tile(i)
            for i in idx:
                if not two_phase:
                    scale_tile(i)
                t = nt_list[i]
                sl = i % nsl
                bank = banks[i % nbk]
                pb = psb16(bank).rearrange("p (a b) -> p a b", b=128)
                pe_tr([(pb[:, kc, :], slots[sl][:, kc * 128:(kc + 1) * 128]) for kc in range(8)],
                      [("hnst", sl), "identb"], PSK(bank))
                c0 = dst_off + i * 128
                for (rap, k0, nk) in runs:
                    tt("dve", rap[:, :, c0:c0 + 128], pb[:, k0:k0 + nk, :],
                       pvec[:, gcol + k0:gcol + k0 + nk].unsqueeze(2).to_broadcast([128, nk, 128]), ALU.mult,
                       ["pvec"], PSK(bank) + [(dstkey, c0 // 512)])

        def norm_to_T(src, nt_list, gcol, dstT, dst_off, srckey, dstkey, stat0):
            for g0 in range(0, len(nt_list), 4):
                norm_group(src, nt_list, gcol, [(dstT, 0, 8)], dst_off, srckey, dstkey, stat0, g0)

        HS4a = [hnst[:, 0, :], hnst[:, 1, :], A2b[:, 16384:17408], A2b[:, 17408:18432]]
        for g0 in range(0, NT, 4):
            norm_group(X, list(range(NT)), 0, [(HT, 0, 8)], 0, "X", "HT", 0, g0, slots=HS4a, banks=(4, 5, 6, 7))
        HTK = [("HT", g) for g in range(4)]
        act(NB[:, :], NB[:, :], AF.Exp, [], [("nab", h) for h in range(8)])
        WAb_ = WA[:, :].bitcast(BF16)
        dma("pool", WAb_[:, 0:4096].rearrange("p (a b) -> p a b", b=512), win_d[:, :, 1024:1536], w=["Wv"])
        dma("pool", WAb_[:, 4096:4224].rearrange("p (a b) -> p a b", b=16), win_d[:, :, 2048:2064], w=["Wg"])
        Sd.barrier()

        def proj_fm(wlist, rhsT, nk, b0, rkeys, gkey=None):
            for g in range(4):
                rk = list(rkeys) + ([(gkey, g)] if gkey is not None else [])
                pe_mms([(PS[:, b0 + g, :], wlist[kc], rhsT[kc][:, g * 512:(g + 1) * 512], kc == 0, kc == nk - 1)
                        for kc in range(nk)], rk, PSK(b0 + g))

        AX = mybir.AxisListType
        Gb = A1[0:16, 0:2048]; SPt = A1[0:16, 2048:4096]; CS = A1[0:16, 4096:6144]
        ZR = A1[0:16, 6144:7168].bitcast(BF16)
        Btok = A1[:, 7168:7296].rearrange("p (a b) -> p a b", b=8)
        Ftok = A1[:, 7296:7424].rearrange("p (a b) -> p a b", b=8)
        BtokT = A1[:, 7168:7296].rearrange("p (t m) -> p m t", m=8)
        FtokT = A1[:, 7296:7424].rearrange("p (t m) -> p m t", m=8)
        EB = A1[:, 7424:7560].rearrange("p (a b) -> p a b", b=17)
        VEC = A1[:, 7560:8072].rearrange("p (c k j) -> p c k j", k=4, j=16)
        Vaug = A1b[:, 16144:24400].rearrange("p (a b c) -> p a b c", b=4, c=129)
        Vfl = A1b[:, 16144:24400].rearrange("p (a c) -> p a c", c=129)
        qTh = A1b[:, 24400:26448]; kTh = A1b[:, 26448:28496]
        sgO2 = [A1b[:, 28496:30544], A2b[:, 38912:40960]]
        kTok = A1b[:, 30544:32592].rearrange("p (a b) -> p a b", b=128)
        mst = A1[:, 16296:16384]
        OO = A1[:, 0:4128].rearrange("p (d j c) -> p d j c", d=2, c=129)
        OOf = A1[:, 0:4128].rearrange("p (a c) -> p a c", c=129)
        hnB = A2b[:, 36864:38912].rearrange("p (a b) -> p a b", b=128)
        Vb = [A1b[:, 8256:10320].rearrange("p (a c) -> p a c", c=129),
              WAb[:, 10368:12432].rearrange("p (a c) -> p a c", c=129)]
        PmT = A1b[:, 11336:12360].rearrange("p (d s c) -> p d s c", d=2, c=128)
        Cf = A1[:, 6180:6438].rearrange("p (d c) -> p d c", c=129)
        Cb = A1b[:, 12876:13656].rearrange("p (d s c) -> p d s c", d=2, c=130)
        CA = A2[:, 12288:14336]; HH = A2[:, 14336:16384]; SQ = A2[:, 16384:18432]
        HH3 = HH.rearrange("p (a b) -> p a b", b=128); SQ3 = SQ.rearrange("p (a b) -> p a b", b=128)
        Wv = WAb[:, 0:4096].rearrange("p (a b) -> p a b", b=512)
        Wg = WAb[:, 4096:4224].rearrange("p (a b) -> p a b", b=16)

        def Wqko(slot, j):
            o0 = 4224 + (slot * 3 + j) * 1024
            return WAb[:, o0:o0 + 1024].rearrange("p (a b) -> p a b", b=128)

        HTl = [HT[:, kc, :] for kc in range(8)]
        for g in range(4):
            pe_mms([(PS[0:16, g, :], Wg[:, kc, :], HT[:, kc, g * 512:(g + 1) * 512], kc == 0, kc == 7) for kc in range(8)],
                   ["Wg"] + HTK, PSK(g))
        P16 = PS[0:16, 0:4, :].rearrange("p a b -> p (a b)")
        act(Gb, P16, AF.Identity, ["gb"], PSK(0, 1, 2, 3) + ["Gb"], bias=gb[:, 0:1])
        Sd.add("pool", lambda e: e.memset(Vfl[:, :, 128:129], 1.0), [], ["Vones"])
        for t in range(NT):
            b = 6 + t % 2
            pe_mms([(PS[:, b, :], HT[:, kc, t * 128:(t + 1) * 128], Wv[:, kc, :], kc == 0, kc == 7) for kc in range(8)],
                   ["Wv"] + HTK, PSK(b))
            act(Vaug[:, t, :, 0:128], PS[:, b, :].rearrange("p (a b) -> p a b", b=128), AF.Copy, [], PSK(b) + [("V", t)])
        act(SPt, Gb, AF.Exp, ["Gb"], ["SPt"], scale=-1.0)
        act(SPt, SPt, AF.Ln, [], ["SPt"], bias=1.0)
        Sd.add("pool", lambda e: e.memset(ZR, 0.0), [], ["ZR"])
        Sd.add("dve", lambda e: e.tensor_tensor_scan(CS, SPt, ZR, 0.0, ALU.add, ALU.add), ["SPt", "ZR"], ["CS"])
        for t in range(NT):
            tsl = slice(t * 128, (t + 1) * 128)
            pe_mms([(PS[:, 4, t * 16:(t + 1) * 16], Gb[:, tsl], cmb[:, 0, :], True, False),
                    (PS[:, 4, t * 16:(t + 1) * 16], CS[:, tsl], cmb[:, 1, :], False, False),
                    (PS[:, 4, t * 16:(t + 1) * 16], SPt[:, tsl], cmb[:, 2, :], False, True)],
                   ["Gb", "CS", "SPt", "cmb"], PSK(4))
        p4 = PS[:, 4, 0:256].rearrange("p (t m) -> p t m", m=16)
        ts("dve", Btok, p4[:, :, 0:8], C0, None, ALU.add, None, [], PSK(4) + ["Btok"])
        Sd.add("dve", lambda e: e.tensor_copy(Ftok, p4[:, :, 8:16]), [], PSK(4) + ["Ftok"])
        ebl = []
        for hd in range(4):
            ebl.append((PS[:, 6, hd * 18 + 1:hd * 18 + 17], sel[:, 1, hd, :], CS[:, 127:2048:128], True, True))
        for hd in range(4):
            c = 4 + hd
            ebl.append((PS[:, 6, c * 18:c * 18 + 16], sel[:, 0, 4 + hd, :], CS[:, 0:2048:128], True, False))
            ebl.append((PS[:, 6, c * 18:c * 18 + 16], sel[:, 1, 4 + hd, :], SPt[:, 0:2048:128], False, True))
            ebl.append((PS[:, 6, c * 18 + 16:c * 18 + 18], sel[:, 0, 4 + hd, :], CS[:, 2046:2048], True, True))
        pe_mms(ebl, ["sel", "CS", "SPt"], PSK(6))
        pe6 = PS[:, 6, 0:144].rearrange("p (a b) -> p a b", b=18)
        Sd.add("pool", lambda e: e.memset(A1[:, 7424:7560], 0.0), [], ["EB"])
        Sd.add("dve", lambda e: e.tensor_copy(EB[:, 0:4, 1:17], pe6[:, 0:4, 1:17]), [], PSK(6) + ["EB"])
        Sd.add("dve", lambda e: e.tensor_copy(EB[:, 4:8, 0:16], pe6[:, 4:8, 0:16]), [], PSK(6) + ["EB"])
        Sd.add("dve", lambda e: e.tensor_copy(EB[:, 4:8, 16:17], pe6[:, 4:8, 17:18]), [], PSK(6) + ["EB"])
        vk = ["EB", "Btok", "Ftok"]
        tt("dve", VEC[:, 0:4, 0, :], FtokT[:, 0:4, :], EB[:, 0:4, 0:16], ALU.subtract, vk, [("VEC", 0)])
        tt("dve", VEC[:, 0:4, 1, :], BtokT[:, 0:4, :], EB[:, 0:4, 0:16], ALU.add, vk, [("VEC", 1)])
        tt("dve", VEC[:, 0:4, 2, :], BtokT[:, 0:4, :], EB[:, 0:4, 1:17], ALU.add, vk, [("VEC", 2)])
        tt("dve", VEC[:, 0:4, 3, :], EB[:, 0:4, 1:17], EB[:, 0:4, 0:16], ALU.subtract, vk, [("VEC", 3)])
        tt("dve", VEC[:, 4:8, 0, :], FtokT[:, 4:8, :], EB[:, 4:8, 1:17], ALU.subtract, vk, [("VEC", 4)])
        tt("dve", VEC[:, 4:8, 1, :], BtokT[:, 4:8, :], EB[:, 4:8, 1:17], ALU.add, vk, [("VEC", 5)])
        tt("dve", VEC[:, 4:8, 2, :], BtokT[:, 4:8, :], EB[:, 4:8, 0:16], ALU.add, vk, [("VEC", 6)])
        tt("dve", VEC[:, 4:8, 3, :], EB[:, 4:8, 0:16], EB[:, 4:8, 1:17], ALU.subtract, vk, [("VEC", 7)])
        act(A1[:, 7560:8072], A1[:, 7560:8072], AF.Exp, [], [("VEC", i) for i in range(8)])
        VK8 = [("VEC", i) for i in range(8)]
        VK = [("V", t) for t in range(NT)] + ["Vones"]
        dma("pool", WAb[:, 0:4096].rearrange("p (a b) -> p a b", b=512), win_d[:, :, 2064:2576], w=[("Wna", 0), "Wv"])
        for j, c0 in enumerate((0, 512, 1536)):
            dma("pool", Wqko(0, j), win_d[:, :, c0:c0 + 128], w=[("Wqko", 0, j)])

        def conv_silu(chunk, dst, dstkey, func, b0=0):
            w0 = pvec[:, 48 + chunk * 3:49 + chunk * 3]; w1 = pvec[:, 49 + chunk * 3:50 + chunk * 3]
            w2 = pvec[:, 50 + chunk * 3:51 + chunk * 3]; bb = pvec[:, 72 + chunk:73 + chunk]
            pf = psf(b0, 4)
            bk = PSK(b0, b0 + 1, b0 + 2, b0 + 3)
            act(CA, pf, AF.Identity, ["pvec"], bk + ["CA"], bias=bb, scale=w1)
            stt("dve", CA[:, 1:2048], pf[:, 0:2047], w0, CA[:, 1:2048], ALU.mult, ALU.add, ["pvec"], bk + ["CA"])
            stt("dve", CA[:, 0:2047], pf[:, 1:2048], w2, CA[:, 0:2047], ALU.mult, ALU.add, ["pvec"], bk + ["CA"])
            act(dst, CA, func, ["CA"], [dstkey])

        def ml_phase_b(hd, part):
            OK_ = [("OO", d, q_) for d in range(2) for q_ in range(4)]
            dnB = mst[:, 0:32]; rdnB = mst[:, 32:64]; hss = mst[:, 64:80]
            aT = VEC[:, hd:8:4, 0, :]
            dn3 = dnB.rearrange("p (d j) -> p d j", d=2); rdn3 = rdnB.rearrange("p (d j) -> p d j", d=2)
            if part == 0:
                act(dnB.unsqueeze(2), OOf[:, :, 128:129], AF.Abs, OK_, ["dnB"])
                tt("dve", dn3, dn3, aT, ALU.mult, VK8, ["dnB"])
                ts("dve", dnB, dnB, 1.0, None, ALU.max, None, [], ["dnB"])
                Sd.add("dve", lambda e: e.reciprocal(rdnB, dnB), ["dnB"], ["rdnB"])
                tt("dve", rdn3, rdn3, aT, ALU.mult, VK8, ["rdnB"])
                tt("dve", HH3, OO[:, 0, :, 0:128], rdnB[:, 0:16].unsqueeze(2).to_broadcast([128, 16, 128]), ALU.mult, OK_ + ["rdnB"], ["HH"])
                tt("dve", SQ3, OO[:, 1, :, 0:128], rdnB[:, 16:32].unsqueeze(2).to_broadcast([128, 16, 128]), ALU.mult, OK_ + ["rdnB"], ["SQ"])
                tt("pool", HH, HH, SQ, ALU.add, ["SQ"], ["HH"])
                tt("pool", SQ, HH, HH, ALU.mult, ["HH"], ["SQ"])
            elif part == 1:
                Sd.add("dve", lambda e: e.tensor_reduce(hss, SQ3, AX.X, ALU.add), ["SQ"], ["hss"])
                ts("dve", hss, hss, 1.0 / 128.0, 1e-6, ALU.mult, ALU.add, [], ["hss"])
                act(hss, hss, AF.Sqrt, [], ["hss"])
                Sd.add("dve", lambda e: e.reciprocal(hss, hss), [], ["hss"])
                tt("dve", hnB, HH3, hss.unsqueeze(2).to_broadcast([128, 16, 128]), ALU.mult, ["HH", "hss"], ["hnB"])
            else:
                for half in range(2):
                    pb = psb16(2 + half).rearrange("p (a b) -> p a b", b=128)
                    pe_tr([(pb[:, j, :], hnB[:, half * 8 + j, :]) for j in range(8)], ["hnB", "identb"], PSK(2 + half))
                    stt("dve", YT[:, hd, half * 1024:(half + 1) * 1024], psb16(2 + half), pvec[:, 168 + hd:169 + hd],
                        sgO2[hd % 2][:, half * 1024:(half + 1) * 1024], ALU.mult, ALU.mult, ["pvec", ("sgO", hd % 2)],
                        PSK(2 + half) + [("YT", hd, 2 * half), ("YT", hd, 2 * half + 1)])

        Wnav = WAb[:, 8192:12288].rearrange("p (a b) -> p a b", b=512)
        for hd in range(4):
            slot = hd % 2
            for j, c0 in enumerate((0, 512, 1536)):
                if hd > 0:
                    dma("pool", Wqko(slot, j), win_d[:, :, c0 + hd * 128:c0 + (hd + 1) * 128], w=[("Wqko", slot, j)])
            proj_fm([Wqko(slot, 0)[:, kc, :] for kc in range(8)], HTl, 8, 0, [("Wqko", slot, 0)] + HTK)
            proj_fm([Wqko(slot, 1)[:, kc, :] for kc in range(8)], HTl, 8, 4, [("Wqko", slot, 1)] + HTK)
            conv_silu(hd, qTh, "qTh", AF.Silu)
            conv_silu(4 + hd, kTh, "kTh", AF.Silu, b0=4)
            proj_fm([Wqko(slot, 2)[:, kc, :] for kc in range(8)], HTl, 8, 0, [("Wqko", slot, 2)] + HTK)
            act(sgO2[hd % 2], psf(0, 4), AF.Sigmoid, [], PSK(0, 1, 2, 3) + [("sgO", hd % 2)])
            if hd == 3:
                S1K = [("Wqko", 1, 0), ("Wqko", 1, 1), ("Wqko", 1, 2)]
                dma("pool", Wnav[:, 0:4, :], win_d[:, 0:4, 3088:3600], w=[("Wna", 2, 0)] + S1K)
                dma("pool", WAb[:, 4096:8192].rearrange("p (a b) -> p a b", b=512), win_d[:, :, 2576:3088],
                    w=[("Wna", 1), "Wg", ("Wqko", 0, 0), ("Wqko", 0, 1), ("Wqko", 0, 2), ("Wqko", 1, 0)])
            if hd > 0:
                ml_phase_b(hd - 1, 0)
            else:
                Sd.barrier()
            for half in range(2):
                pb = psb16(2 + half).rearrange("p (a b) -> p a b", b=128)
                pe_tr([(pb[:, j, :], kTh[:, (half * 8 + j) * 128:(half * 8 + j + 1) * 128]) for j in range(8)], ["kTh", "identb"], PSK(2 + half))
                act(kTok[:, half * 8:(half + 1) * 8, :], pb[:, :, :], AF.Copy, [], PSK(2 + half) + [("kTok", half)])
            KTK = [("kTok", 0), ("kTok", 1)]
            if hd == 0:
                for dr in range(2):
                    c = dr * 4 + hd
                    tt("dve", Vb[dr], Vaug[:, :, hd, :], VEC[:, c, 1, :].unsqueeze(2).to_broadcast([128, 16, 129]), ALU.mult,
                       VK + VK8, [("Vb", dr)])
            Sd.add("pool", lambda e: e.memset(A1[:, 6180:6438], 0.0), [], [("Cf", 0), ("Cf", 1)])
            Sd.add("pool", lambda e: e.memset(A1b[:, 12876:13656], 0.0), [], [("Cb", d, p_) for d in range(2) for p_ in range(3)])
            def ml_a(step):
                par = step % 2
                bX = 4 + par; bY = 6 + par
                Js = (step, NT - 1 - step)
                lst = []
                for dr in range(2):
                    J = Js[dr]; Jsl = slice(J * 128, (J + 1) * 128)
                    lst.append((PS[:, bX, dr * 128:(dr + 1) * 128], kTh[:, Jsl], qTh[:, Jsl], True, True))
                pe_mms(lst, ["kTh", "qTh"], PSK(bX))
                pe_mms([(PS[:, bX, 256:385], kTok[:, Js[0], :], Vb[0][:, Js[0], :], True, True)], [("Vb", 0)] + KTK, PSK(bX))
                pe_mms([(PS[:, bY, 0:129], kTok[:, Js[1], :], Vb[1][:, Js[1], :], True, True)], [("Vb", 1)] + KTK, PSK(bY))
                tt("dve", PmT[:, :, par, :], PS[:, bX, 0:256].rearrange("p (a b) -> p a b", b=128), tri[:, :, :], ALU.mult,
                   ["tri"], PSK(bX) + [("PmT", 0, par), ("PmT", 1, par)])
                if step < NT - 1:
                    for dr in range(2):
                        J = Js[dr]; c = dr * 4 + hd
                        Jp = J - 1 if dr == 0 else J + 1
                        gp = VEC[:, c, 3, Jp:Jp + 1] if step > 0 else VEC[:, c, 3, J:J + 1]
                        usrc = PS[:, bX, 256:385] if dr == 0 else PS[:, bY, 0:129]
                        stt("dve", Cf[:, dr, :], Cf[:, dr, :], gp, usrc, ALU.mult, ALU.add,
                            VK8, PSK(bX if dr == 0 else bY) + [("Cf", dr)])
                        act(Cb[:, dr, (step + 1) % 3, 0:129], Cf[:, dr, :], AF.Copy, [("Cf", dr)] + VK8, [("Cb", dr, (step + 1) % 3)],
                            scale=VEC[:, c, 3, J:J + 1])

            def ml_b(step):
                par = step % 2
                bY = 6 + par
                Js = (step, NT - 1 - step)
                lst = []
                for dr in range(2):
                    J = Js[dr]; Jsl = slice(J * 128, (J + 1) * 128)
                    oc = 129 + dr * 129
                    lst.append((PS[:, bY, oc:oc + 129], PmT[:, dr, par, :], Vb[dr][:, J, :], True, False))
                    lst.append((PS[:, bY, oc:oc + 129], qTh[:, Jsl], Cb[:, dr, step % 3, 0:129], False, True))
                pe_mms(lst, [("PmT", 0, par), ("PmT", 1, par), "qTh", ("Cb", 0, step % 3), ("Cb", 1, step % 3), ("Vb", 0), ("Vb", 1)], PSK(bY))
                act(OOf[:, step:32 - step:31 - 2 * step, :], PS[:, bY, 129:387].rearrange("p (a b) -> p a b", b=129), AF.Copy, [],
                    PSK(bY) + [("OO", 0, Js[0] // 4), ("OO", 1, Js[1] // 4)])

            ml_a(0)
            for step in range(NT):
                if step + 1 < NT:
                    ml_a(step + 1)
                ml_b(step)
                if hd > 0 and step == 4:
                    ml_phase_b(hd - 1, 1)
                if hd > 0 and step == 9:
                    ml_phase_b(hd - 1, 2)
            if hd < 3:
                for dr in range(2):
                    c = dr * 4 + hd + 1
                    tt("dve", Vb[dr], Vaug[:, :, hd + 1, :], VEC[:, c, 1, :].unsqueeze(2).to_broadcast([128, 16, 129]), ALU.mult,
                       VK + VK8, [("Vb", dr)])
        Vna = A1b[:, 16384:24704].rearrange("p (a b c) -> p a b c", b=8, c=65)
        Vnf = A1b[:, 16384:24704].rearrange("p (a c) -> p a c", c=65)
        dma("pool", Wnav[:, 4:8, :], win_d[:, 4:8, 3088:3600], w=[("Wna", 2, 1), ("Wqko", 1, 2), ("Vb", 1)])
        VALIAS = VK + ["qTh"]
        Sd.add("pool", lambda e: e.memset(Vnf[:, :, 64:65], 1.0), [], ["Vnones"] + VALIAS)
        ml_phase_b(3, 0)
        for t in range(NT):
            if t == 5:
                ml_phase_b(3, 1)
            if t == 10:
                ml_phase_b(3, 2)
            b = 4 + t % 2
            pe_mms([(PS[:, b, :], HT[:, kc, t * 128:(t + 1) * 128], Wnav[:, kc, :], kc == 0, kc == 7) for kc in range(8)],
                   [("Wna", 2, 0), ("Wna", 2, 1)] + HTK, PSK(b))
            act(Vna[:, t, :, 0:64], PS[:, b, :].rearrange("p (a b) -> p a b", b=64), AF.Copy, [], PSK(b) + [("Vn", t)] + VALIAS)
        Sd.barrier()

        qTn = A1b[:, 0:8192].rearrange("p (a b) -> p a b", b=2048)
        kTn = A1b[:, 8192:16384].rearrange("p (a b) -> p a b", b=2048)
        Vna = A1b[:, 16384:24704].rearrange("p (a b c) -> p a b c", b=8, c=65)
        Vnf = A1b[:, 16384:24704].rearrange("p (a c) -> p a c", c=65)
        pTn = A1b[:, 24704:27264].rearrange("p (s h b) -> p s h b", s=2, h=2)
        pEx = A1b[:, 27264:29824].rearrange("p (s h b) -> p s h b", s=2, h=2)
        onb = A1b[:, 29824:30336]
        rdna = A1[:, 15168:15176]
        Wna3 = [WAb[:, j * 4096:(j + 1) * 4096].rearrange("p (a b) -> p a b", b=512) for j in range(3)]
        for c in range(4):
            proj_fm([Wna3[0][:, kc, c * 128:(c + 1) * 128] for kc in range(8)], HTl, 8, 4 * (c % 2), [("Wna", 0)] + HTK)
            b0 = 4 * (c % 2)
            act(qTn[:, c, :], psf(b0, 4), AF.Copy, [], PSK(b0, b0 + 1, b0 + 2, b0 + 3) + [("qTn", c)], scale=0.125)
        for c in range(4):
            b0 = 4 * (c % 2)
            proj_fm([Wna3[1][:, kc, c * 128:(c + 1) * 128] for kc in range(8)], HTl, 8, b0, [("Wna", 1)] + HTK)
            Sd.add("dve", lambda e, c=c, b0=b0: e.tensor_copy(kTn[:, c, :], psf(b0, 4)), [], PSK(b0, b0 + 1, b0 + 2, b0 + 3) + [("kTn", c)])
        NABK = [("nab", h) for h in range(8)]
        VNK = [("Vn", t) for t in range(NT)] + ["Vnones"]

        def na_keys(t):
            if t == 0:
                return [0, 1, 2, 3], [(4, 2), (7, 2)]
            if t == 1:
                return [0, 1, 2, 3], [(3, 3), (7, 1)]
            if t == 14:
                return [12, 13, 14, 15], [(1, 1), (3, 3)]
            if t == 15:
                return [12, 13, 14, 15], [(0, 2), (3, 2)]
            return [t - 2, t - 1, t, t + 1, t + 2], [(2, 5)]

        def na_sc(sl, hp, i):
            col = hp * 640 + i * 128
            return PS[:, 3 * sl + col // 512, col % 512:col % 512 + 128]

        def na_qk(t, c):
            tsl = slice(t * 128, (t + 1) * 128)
            kts, runs = na_keys(t)
            n = len(kts)
            sl = c % 2
            bk = PSK(3 * sl, 3 * sl + 1, 3 * sl + 2)
            lst = []
            for i, kt in enumerate(kts):
                for hp in range(2):
                    lst.append((na_sc(sl, hp, i), kTn[hp * 64:(hp + 1) * 64, c, kt * 128:(kt + 1) * 128],
                                qTn[hp * 64:(hp + 1) * 64, c, tsl], True, True))
            pe_mms(lst, [("qTn", c), ("kTn", c)], bk)

        def na_ew(t, c, hps):
            kts, runs = na_keys(t)
            n = len(kts)
            sl = c % 2
            bk = PSK(3 * sl, 3 * sl + 1, 3 * sl + 2)
            pin = psf(3 * sl, 3)[:, 0:1280].rearrange("p (h b) -> p h b", h=2)
            for hp in hps:
                act(pEx[:, sl, hp, 0:n * 128], pin[:, hp, 0:n * 128], AF.Exp, [], bk + [("pEx", sl, hp)])
                off = 0
                for (e0, m) in runs:
                    tt("dve", pTn[:, sl, hp, off * 128:(off + m) * 128], pEx[:, sl, hp, off * 128:(off + m) * 128],
                       nab[:, 2 * c + hp, e0:e0 + m, :].rearrange("p e k -> p (e k)"), ALU.mult,
                       [("pEx", sl, hp)] + NABK, [("pTn", sl, hp)])
                    off += m

        def na_out(h):
            return (6, h * 65) if h < 7 else (7, 0)

        def na_pv(t, c, hp):
            tsl = slice(t * 128, (t + 1) * 128)
            kts, runs = na_keys(t)
            n = len(kts)
            sl = c % 2
            h = 2 * c + hp
            ob, oc = na_out(h)
            pe_mms([(PS[:, ob, oc:oc + 65], pTn[:, sl, hp, i * 128:(i + 1) * 128], Vna[:, kt, h, :], i == 0, i == n - 1)
                    for i, kt in enumerate(kts)], [("pTn", sl, hp)] + VNK, PSK(ob))
            if h < 7:
                return
            pv6 = PS[:, 6, 0:455].rearrange("p (a b) -> p a b", b=65)
            Sd.add("dve", lambda e, pv6=pv6: e.reciprocal(rdna[:, 0:7].unsqueeze(2), pv6[:, :, 64:65]), [], PSK(6) + [("rdna", 0)])
            Sd.add("dve", lambda e: e.reciprocal(rdna[:, 7:8], PS[:, 7, 64:65]), [], PSK(7) + [("rdna", 1)])
            tt("dve", onb[:, 0:448].rearrange("p (a b) -> p a b", b=64), pv6[:, :, 0:64],
               rdna[:, 0:7].unsqueeze(2).to_broadcast([128, 7, 64]), ALU.mult, [("rdna", 0)], PSK(6) + [("onb", 0)])
            ts("dve", onb[:, 448:512], PS[:, 7, 0:64], rdna[:, 7:8], None, ALU.mult, None, [("rdna", 1)], PSK(7) + [("onb", 1)])
            pb = psb16(7)[:, 512:1024].rearrange("p (a b) -> p a b", b=128)
            pe_tr([(pb[:, cc, :], onb[:, cc * 128:(cc + 1) * 128]) for cc in range(4)], [("onb", 0), ("onb", 1), "identb"], PSK(7))
            Sd.add("act", lambda e, pb=pb, tsl=tsl: e.activation(out=YT[:, 4:8, tsl], in_=pb[:, 0:4, :], func=AF.Copy), [],
                   PSK(7) + [("YT", 4 + cc, t // 4) for cc in range(4)])

        Wxq = WAb[:, 0:4096].rearrange("p (a b) -> p a b", b=512)
        Wkv = WAb[:, 4096:12288].rearrange("p (a b) -> p a b", b=1024)
        WNAK = [("Wna", 0), ("Wna", 1), ("Wna", 2, 0), ("Wna", 2, 1)]
        dma("pool", Wxq, win_d[:, :, 3600:4112], w=["Wxq"] + WNAK)
        for j in range(2):
            dma("pool", Wkv[:, :, j * 512:(j + 1) * 512], wkv_d[:, :, j * 512:(j + 1) * 512], w=[("Wkv", j)] + WNAK)
        items = [(t, c) for t in range(NT) for c in range(4)]
        for i0_ in (0, 1):
            na_qk(*items[i0_]); na_ew(items[i0_][0], items[i0_][1], (0, 1))
        for i in range(len(items)):
            na_pv(items[i][0], items[i][1], 0)
            if i + 2 < len(items):
                na_qk(*items[i + 2]); na_ew(items[i + 2][0], items[i + 2][1], (0,))
            na_pv(items[i][0], items[i][1], 1)
            if i + 2 < len(items):
                na_ew(items[i + 2][0], items[i + 2][1], (1,))
        fg = NB[:, 0:2048].bitcast(F32)
        Sd.barrier()
        dma("sp", fg, fg_d, w=["fg"])

        qTx = A1b[:, 0:8192].rearrange("p (a b) -> p a b", b=2048)
        memst = A1[:, 4096:6144].rearrange("p (a b) -> p a b", b=1024)
        MHT = A1b[:, 14336:16384].rearrange("p (a b) -> p a b", b=256)
        KxT = A1b[:, 16384:17408].rearrange("p (a b) -> p a b", b=256)
        Vx = A1b[:, 17408:18440].rearrange("p (a b c) -> p a b c", b=4, c=129)
        Vxf = A1b[:, 17408:18440].rearrange("p (a c) -> p a c", c=129)
        pTx = A1b[:, 19472:21520].rearrange("p (a b) -> p a b", b=1024)
        oxb = A1b[:, 18952:19464]
        rdx = A1[:, 9732:9736]
        for mt in range(2):
            dma("sp", memst[:, mt, :], mem_d[mt * 128:(mt + 1) * 128, :], w=[("memst", mt)])
        norm_to_T(memst, [0, 1], 16, MHT, 0, "memst", "MHT", 80)
        MK = [("MHT", 0)]
        for hd in range(4):
            b = hd // 2
            pe_mms([(PS[:, b, (hd % 2) * 256:(hd % 2 + 1) * 256], Wkv[:, kc, hd * 128:(hd + 1) * 128], MHT[:, kc, :], kc == 0, kc == 7)
                    for kc in range(8)], [("Wkv", 0)] + MK, PSK(b))
        act(A1b[:, 16384:17408], psf(0, 2), AF.Copy, [], PSK(0, 1) + ["KxT"], scale=float(128.0 ** -0.5))
        Sd.add("pool", lambda e: e.memset(Vxf[:, :, 128:129], 1.0), [], ["Vxones"])
        for mt in range(2):
            pe_mms([(PS[:, 2 + mt, :], MHT[:, kc, mt * 128:(mt + 1) * 128], Wkv[:, kc, 512:1024], kc == 0, kc == 7)
                    for kc in range(8)], [("Wkv", 1)] + MK, PSK(2 + mt))
            act(Vx[:, mt, :, 0:128], PS[:, 2 + mt, :].rearrange("p (a b) -> p a b", b=128), AF.Copy, [], PSK(2 + mt) + [("Vx", mt)])
        for hd in range(4):
            b0 = 4 * (hd % 2)
            proj_fm([Wxq[:, kc, hd * 128:(hd + 1) * 128] for kc in range(8)], HTl, 8, b0, ["Wxq"] + HTK)
            if hd % 2 == 0:
                act(qTx[:, hd, :], psf(b0, 4), AF.Copy, [], PSK(b0, b0 + 1, b0 + 2, b0 + 3) + [("qTx", hd)])
            else:
                Sd.add("dve", lambda e, hd=hd, b0=b0: e.tensor_copy(qTx[:, hd, :], psf(b0, 4)), [], PSK(b0, b0 + 1, b0 + 2, b0 + 3) + [("qTx", hd)])
        VXK = [("Vx", 0), ("Vx", 1), "Vxones"]

        def xa_qk(t):
            tsl = slice(t * 128, (t + 1) * 128)
            sl = t % 2
            sb_ = 2 * sl
            pf2 = psf(sb_, 2)
            pe_mms([(pf2[:, hd * 256 + mt * 128:hd * 256 + (mt + 1) * 128], KxT[:, hd, mt * 128:(mt + 1) * 128], qTx[:, hd, tsl], True, True)
                    for hd in range(4) for mt in range(2)], [("qTx", hd) for hd in range(4)] + ["KxT"], PSK(sb_, sb_ + 1))
            act(pTx[:, sl, :], pf2, AF.Exp, [], PSK(sb_, sb_ + 1) + [("pTx", sl)])

        def xa_pv(t):
            tsl = slice(t * 128, (t + 1) * 128)
            sl = t % 2
            for half in range(2):
                lst = []
                for hd in (2 * half, 2 * half + 1):
                    oc = (hd % 2) * 129
                    for mt in range(2):
                        lst.append((PS[:, 4 + half, oc:oc + 129], pTx[:, sl, hd * 256 + mt * 128:hd * 256 + (mt + 1) * 128],
                                    Vx[:, mt, hd, :], mt == 0, mt == 1))
                pe_mms(lst, [("pTx", sl)] + VXK, PSK(4 + half))

        def xa_epi(t):
            tsl = slice(t * 128, (t + 1) * 128)
            for half in range(2):
                pv = PS[:, 4 + half, 0:258].rearrange("p (a b) -> p a b", b=129)
                rv = rdx[:, half * 2:(half + 1) * 2].unsqueeze(2)
                Sd.add("dve", lambda e, rv=rv, pv=pv: e.reciprocal(rv, pv[:, :, 128:129]), [], PSK(4 + half) + [("rdx", half)])
                tt("dve", oxb[:, half * 256:(half + 1) * 256].rearrange("p (a b) -> p a b", b=128), pv[:, :, 0:128],
                   rdx[:, half * 2:(half + 1) * 2].unsqueeze(2).to_broadcast([128, 2, 128]), ALU.mult,
                   [("rdx", half)], PSK(4 + half) + [("oxb", half)])
            tb = 6 + t % 2
            pb = psb16(tb).rearrange("p (a b) -> p a b", b=128)
            pe_tr([(pb[:, c, :], oxb[:, c * 128:(c + 1) * 128]) for c in range(4)], [("oxb", 0), ("oxb", 1), "identb"], PSK(tb))
            Sd.add("act", lambda e, pb=pb, tsl=tsl: e.activation(out=YT[:, 8:12, tsl], in_=pb[:, 0:4, :], func=AF.Copy), [],
                   PSK(tb) + [("YT", 8 + c, t // 4) for c in range(4)])

        XWK = ["Wxq", ("Wkv", 0), ("Wkv", 1)]
        for br in range(3):
            c0 = 4112 + br * 1024
            dma("pool", WAb[:, br * 1024:(br + 1) * 1024].rearrange("p (a b) -> p a b", b=128), win_d[:, :, c0:c0 + 128],
                w=[("Wmg", 0, br)] + XWK)
            dma("pool", WAb[:, 3072 + br * 512:3072 + (br + 1) * 512].rearrange("p (a b) -> p a b", b=128), wbr_d[br, :, :, 0:128],
                w=[("Wmb", 0, br)] + XWK)
        xa_qk(0); xa_qk(1)
        for t in range(NT):
            xa_pv(t)
            if t + 2 < NT:
                xa_qk(t + 2)
            xa_epi(t)
        Sd.barrier()

        MT = A1b[:, 0:16384].rearrange("p (a b) -> p a b", b=2048)
        SG = A1[:, 8192:10240]; ACm = A1[:, 10240:12288]

        def Wmg(slot, br):
            o0 = slot * 4608 + br * 1024
            return WAb[:, o0:o0 + 1024].rearrange("p (a b) -> p a b", b=128)

        def Wmb(slot, br):
            o0 = slot * 4608 + 3072 + br * 512
            return WAb[:, o0:o0 + 512].rearrange("p (a b) -> p a b", b=128)

        Wo = A1b[:, 24576:32768].rearrange("p (a b) -> p a b", b=1024)
        for j in range(2):
            dma("pool", Wo[:, :, j * 512:(j + 1) * 512], wout_d[:, :, j * 512:(j + 1) * 512], w=[("Wo", j)])
        for f in range(8):
            slot = f % 2
            for br in range(3):
                if f == 0:
                    continue
                c0 = 4112 + br * 1024 + f * 128
                dma("pool", Wmg(slot, br), win_d[:, :, c0:c0 + 128], w=[("Wmg", slot, br)])
                dma("pool", Wmb(slot, br), wbr_d[br, :, :, f * 128:(f + 1) * 128], w=[("Wmb", slot, br)])
            for br in range(3):
                proj_fm([Wmg(slot, br)[:, kc, :] for kc in range(8)], HTl, 8, 0, [("Wmg", slot, br)] + HTK)
                act(SG, psf(0, 4), AF.Sigmoid, ["pvec"], PSK(0, 1, 2, 3) + ["SG"], bias=pvec[:, 24 + br * 8 + f:25 + br * 8 + f])
                proj_fm([Wmb(slot, br)[:, kc, :] for kc in range(4)], [YT[:, br * 4 + kc, :] for kc in range(4)], 4, 4,
                        [("Wmb", slot, br)] + [("YT", br * 4 + kc, g) for kc in range(4) for g in range(4)])
                if br == 0:
                    tt("dve", ACm, psf(4, 4), SG, ALU.mult, ["SG"], PSK(4, 5, 6, 7) + ["ACm"])
                else:
                    tt("dve", SG, psf(4, 4), SG, ALU.mult, [], PSK(4, 5, 6, 7) + ["SG"])
                    if br == 1:
                        tt("dve", ACm, ACm, SG, ALU.add, ["SG"], ["ACm"])
                    else:
                        tt("dve", MT[:, f, :], ACm, SG, ALU.add, ["SG", "ACm"], [("MT", f)])
        if debug:
            dma("sp", dbg_d["HT"], HT, r=HTK)
            dma("sp", dbg_d["YT"], YT, r=[("YT", a, g) for a in range(12) for g in range(4)])
            dma("sp", dbg_d["MT"], MT, r=[("MT", f) for f in range(8)])
        Sd.barrier()

        xs = A2[:, 16384:18432].rearrange("p (a b) -> p a b", b=1024)
        H2runs = [(A1b[:, 16384:24576].rearrange("p (a b) -> p a b", b=2048), 0, 4),
                  (A2b[:, 36864:40960].rearrange("p (a b) -> p a b", b=2048), 4, 2),
                  (NB[:, 2048:6144].rearrange("p (a b) -> p a b", b=2048), 6, 2)]
        H2l = [A1b[:, 16384 + k * 2048:16384 + (k + 1) * 2048] for k in range(4)] + \
              [A2b[:, 36864 + k * 2048:36864 + (k + 1) * 2048] for k in range(2)] + \
              [NB[:, 2048 + k * 2048:2048 + (k + 1) * 2048] for k in range(2)]
        MTK = [("MT", f) for f in range(8)]
        HS4 = [hnst[:, 0, :], hnst[:, 1, :], NB[:, 6144:7168], NB[:, 7168:8192]]
        for t in range(NT):
            tsl = slice(t * 128, (t + 1) * 128)
            sl = t % 2
            dma("sp", xs[:, sl, :], x_d[t * 128:(t + 1) * 128, :], w=[("xs", sl)])
            b0 = 2 * sl
            for n in range(2):
                pe_mms([(PS[:, b0 + n, :], MT[:, f, tsl], Wo[:, f, n * 512:(n + 1) * 512], f == 0, f == 7) for f in range(8)],
                       MTK + [("Wo", n)], PSK(b0 + n))
            tt("dve", X1[:, t, :], psf(b0, 2), xs[:, sl, :], ALU.add, [("xs", sl)], PSK(b0, b0 + 1) + [("X1", t)])
            if t % 4 == 3 and t >= 7:
                norm_group(X1, list(range(NT)), 8, H2runs, 0, "X1", "H2T", 112, t - 7, slots=HS4, banks=(4, 5, 6, 7))
        norm_group(X1, list(range(NT)), 8, H2runs, 0, "X1", "H2T", 112, 12, slots=HS4, banks=(4, 5, 6, 7))
        if debug:
            dma("sp", dbg_d["X1"], X1, r=[("X1", t) for t in range(NT)])
        dma("pool", WAb[:, 0:1024].rearrange("p (a b) -> p a b", b=128), wup_d[:, :, 0:128], w=[("Wau", 0, 0)])
        dma("pool", WAb[:, 1024:2048].rearrange("p (a b) -> p a b", b=128), wup_d[:, :, 2816:2944], w=[("Wau", 0, 1)])
        dma("pool", WAb[:, 4096:5120], wdn_d[:, 0, :], w=[("Wd", 0)])

        ATl = [A1b[:, i * 2048:(i + 1) * 2048] for i in range(8)]
        CA2 = A2[:, 16384:18432]; CB2 = A1[:, 12288:14336]
        CA2K = ["CA2", ("xs", 0), ("xs", 1)]; CB2K = ["CB2", ("Wo", 0), ("Wo", 1)]
        H2K = [("H2T", g) for g in range(4)]

        def Wau(slot, j):
            o0 = (slot * 2 + j) * 1024
            return WAb[:, o0:o0 + 1024].rearrange("p (a b) -> p a b", b=128)

        def Wd(i):
            return WAb[:, 4096 + i * 1024:4096 + (i + 1) * 1024]

        ssq = st[:, 160:176]; ms = st[:, 176:192]; rs = st[:, 192:208]
        GF = 2

        def final_group(g0):
            gs = slice(g0, g0 + GF)
            for t in range(g0, g0 + GF):
                act(junk[:, :], X1[:, t, :], AF.Square, [("X1", t)], [("ssq3", t), "junk"], accum=ssq[:, t:t + 1])
            ts("dve", ms[:, gs], ssq[:, gs], 1.0 / 1024.0, 1e-6, ALU.mult, ALU.add, [("ssq3", t) for t in range(g0, g0 + GF)], [("rs3t", g0)])
            act(ms[:, gs], ms[:, gs], AF.Sqrt, [], [("rs3t", g0)])
            Sd.add("dve", lambda e, gs=gs: e.reciprocal(rs[:, gs], ms[:, gs]), [("rs3t", g0)], [("rs3", g0)])
            for t in range(g0, g0 + GF):
                if t % 3 != 2:
                    stt("dve", X1[:, t, :], X1[:, t, :], rs[:, t:t + 1], fg, ALU.mult, ALU.mult, [("rs3", g0), "fg"], [("X1", t)])
                else:
                    act(X1[:, t, :], X1[:, t, :], AF.Copy, [("rs3", g0)], [("X1", t)], scale=rs[:, t:t + 1])
                    tt("pool", X1[:, t, :], X1[:, t, :], fg, ALU.mult, ["fg"], [("X1", t)])
                dma("sp", y_d[t * 128:(t + 1) * 128, :], X1[:, t, :], r=[("X1", t)])

        groups = [list(range(0, 8)), list(range(8, 16)), list(range(16, 22))]
        ucount = 0
        for clist in groups:
            for i, c in enumerate(clist):
                slot = ucount % 2; ucount += 1
                if c > 0:
                    dma("pool", Wau(slot, 0), wup_d[:, :, c * 128:(c + 1) * 128], w=[("Wau", slot, 0)])
                    dma("pool", Wau(slot, 1), wup_d[:, :, 2816 + c * 128:2816 + (c + 1) * 128], w=[("Wau", slot, 1)])
                    dma("pool", Wd(i), wdn_d[:, c, :], w=[("Wd", i)])
                proj_fm([Wau(slot, 0)[:, kc, :] for kc in range(8)], H2l, 8, 0, [("Wau", slot, 0)], gkey="H2T")
                proj_fm([Wau(slot, 1)[:, kc, :] for kc in range(8)], H2l, 8, 4, [("Wau", slot, 1)], gkey="H2T")
                w0 = pvec[:, 80 + c * 3:81 + c * 3]; w1 = pvec[:, 81 + c * 3:82 + c * 3]; w2 = pvec[:, 82 + c * 3:83 + c * 3]
                bb = pvec[:, 146 + c:147 + c]
                pf = psf(0, 4)
                act(CA2, pf, AF.Identity, ["pvec"], PSK(0, 1, 2, 3) + CA2K, bias=bb, scale=w1)
                stt("dve", CA2[:, 1:2048], pf[:, 0:2047], w0, CA2[:, 1:2048], ALU.mult, ALU.add, ["pvec"], PSK(0, 1, 2, 3) + CA2K)
                stt("dve", CA2[:, 0:2047], pf[:, 1:2048], w2, CA2[:, 0:2047], ALU.mult, ALU.add, ["pvec"], PSK(0, 1, 2, 3) + CA2K)
                act(CB2, CA2, AF.Gelu_apprx_tanh, CA2K, CB2K)
                tt("dve", ATl[i], CB2, psf(4, 4), ALU.mult, CB2K, PSK(4, 5, 6, 7) + [("MT", i)])
            nk = len(clist)
            for t in range(NT):
                tsl = slice(t * 128, (t + 1) * 128)
                b0 = 2 * (t % 2)
                for n in range(2):
                    pe_mms([(PS[:, b0 + n, :], ATl[i][:, tsl], Wd(i)[:, n * 512:(n + 1) * 512], i == 0, i == nk - 1) for i in range(nk)],
                           [("MT", i) for i in range(nk)] + [("Wd", i) for i in range(nk)], PSK(b0 + n))
                tt("dve", X1[:, t, :], psf(b0, 2), X1[:, t, :], ALU.add, [], PSK(b0, b0 + 1) + [("X1", t)])
                if clist is groups[-1] and t % GF == GF - 1:
                    final_group(t - GF + 1)

        Sd.barrier()
        Sd.add("sp", lambda e: e.nop(), [], [])

        Sd.finalize()

        @block.tensor
        def _(e):
            Sd.emit("pe", e, sems, dsems)

        @block.scalar
        def _(e):
            Sd.emit("act", e, sems, dsems)

        @block.vector
        def _(e):
            Sd.emit("dve", e, sems, dsems)

        @block.gpsimd
        def _(e):
            Sd.emit("pool", e, sems, dsems)

        @block.sync
        def _(e):
            Sd.emit("sp", e, sems, dsems)
    return nc


def _nab_table(rpb):
    kk = np.arange(128); qq = np.arange(128)
    kl = kk // 64; kc = kk % 64; ql = qq // 64; qc = qq % 64
    col_start = np.clip(qc - 8, 0, 48)
    col_ok = (kc[:, None] >= col_start[None, :]) & (kc[:, None] < col_start[None, :] + 16)
    dc = np.clip(kc[:, None] - qc[None, :] + 15, 0, 30)
    out = np.empty((128, 8, 9, 128), np.float32)
    for e, d2 in enumerate([-3, -2, -2, -1, 0, 1, 2, 2, 3]):
        dr = 2 * d2 + kl[:, None] - ql[None, :] + 7
        ok = col_ok & (dr >= 0) & (dr <= 14)
        if e == 2:
            ok = ok & ~((ql[None, :] == 1) & (kl[:, None] == 0))
        if e == 6:
            ok = ok & ((ql[None, :] == 1) & (kl[:, None] == 0))
        drc = np.clip(dr, 0, 14)
        vals = rpb[:, drc, dc]
        out[:, :, e, :] = np.where(ok[None], vals, np.float32(-30000.0)).transpose(1, 0, 2)
    return out


def _pmaj(v, n):
    return np.ascontiguousarray(np.asarray(v, np.float32).reshape(n, 128).T)


def _wl(w, kc):
    w = np.asarray(w, np.float32)
    return np.ascontiguousarray(w.reshape(kc, 128, w.shape[1]).transpose(1, 0, 2))


def _host_layout(inp):
    L = 0
    sh = {}
    sh["w_in"] = _wl(inp["w_in"][L], 8)
    sh["w_kv"] = _wl(inp["w_mem_kv"][L], 8)
    sh["w_br"] = np.ascontiguousarray(np.stack([_wl(inp[k][L], 4) for k in ("w_br_ml", "w_br_na", "w_br_xa")]))
    sh["w_out"] = _wl(inp["w_out"][L], 8)
    sh["w_up"] = _wl(inp["w_ffn_up"][L], 8)
    sh["w_dn"] = _wl(inp["w_ffn_down"][L], 22)
    pv = np.zeros((128, NPV), np.float32)
    pv[:, 0:8] = _pmaj(inp["mix_norm_g"][L], 8)
    pv[:, 8:16] = _pmaj(inp["ffn_norm_g"][L], 8)
    pv[:, 16:24] = _pmaj(inp["mem_norm_g"][L], 8)
    pv[:, 24:48] = _pmaj(inp["b_merge_gate"][L], 24)
    wc = np.asarray(inp["w_ml_conv"][L], np.float32)
    pv[:, 48:72] = wc.reshape(3, 8, 128).transpose(2, 1, 0).reshape(128, 24)
    pv[:, 72:80] = _pmaj(inp["b_ml_conv"][L], 8)
    wf = np.asarray(inp["w_ffn_conv"][L], np.float32)
    pv[:, 80:146] = wf.reshape(3, 22, 128).transpose(2, 1, 0).reshape(128, 66)
    pv[:, 146:168] = _pmaj(inp["b_ffn_conv"][L], 22)
    pv[:, 168:172] = _pmaj(inp["ml_norm_g"][L], 4)
    sh["pvec"] = pv
    sh["fg"] = np.ascontiguousarray(np.broadcast_to(np.asarray(inp["final_norm_g"], np.float32)[None, :], (128, 1024)))
    big = np.asarray(inp["b_ml_igate"][L], np.float32); bfg = np.asarray(inp["b_ml_fgate"][L], np.float32)
    sh["gb"] = np.ascontiguousarray(np.concatenate([big[0], bfg[0], big[1], bfg[1]]).reshape(16, 1))
    sh["ident"] = np.eye(128, dtype=np.float32)
    tri = np.zeros((128, 2, 128), np.float32)
    s_ = np.arange(128)[:, None]; j_ = np.arange(128)[None, :]
    tri[:, 0, :] = (s_ <= j_); tri[:, 1, :] = (s_ >= j_)
    sh["tri"] = tri
    sel = np.zeros((16, 2, 8, 128), np.float32)
    for i in range(8):
        row = 4 + i if i < 4 else 8 + i
        sel[row, 0, i, :] = 1.0; sel[row, 1, i, :] = -1.0
    sh["sel"] = sel
    cmb = np.zeros((16, 3, 16), np.float32)
    for hd in range(4):
        cmb[hd, 0, hd] = 1.0; cmb[8 + hd, 0, 4 + hd] = 1.0
        cmb[4 + hd, 1, hd] = 1.0; cmb[12 + hd, 1, 4 + hd] = -1.0
        cmb[12 + hd, 2, 4 + hd] = 1.0
        cmb[4 + hd, 1, 8 + hd] = -1.0; cmb[12 + hd, 1, 8 + 4 + hd] = 1.0
        cmb[12 + hd, 2, 8 + 4 + hd] = -1.0
    sh["cmb"] = cmb
    sh["nab"] = _nab_table(np.asarray(inp["na_rpb"][L], np.float32))
    return sh


_NC_CACHE = {}


def kernel(**inputs):
    inp = {k: np.asarray(v) for k, v in inputs.items()}
    shared = _host_layout(inp)
    x = np.asarray(inp["x"], np.float32); mem = np.asarray(inp["mem"], np.float32)
    if "nc" not in _NC_CACHE:
        _NC_CACHE["nc"] = build(False)
    nc = _NC_CACHE["nc"]
    in_maps = []
    for b in range(8):
        m = dict(shared)
        m["x"] = np.ascontiguousarray(x[b]); m["mem"] = np.ascontiguousarray(mem[b])
        in_maps.append(m)
    res = run_bass_kernel_spmd(nc, in_maps, core_ids=list(range(8)))
    return np.stack([np.asarray(r["y"], np.float32).reshape(2048, 1024) for r in res.results], axis=0)
```

```python
import math
import numpy as np
import concourse.bass as bass
import concourse.mybir as mybir
from concourse.bass_utils import run_bass_kernel_spmd

F32 = mybir.dt.float32
BF16 = mybir.dt.bfloat16
AF = mybir.ActivationFunctionType
ALU = mybir.AluOpType

S = 2048
NT = 16
NPV = 172
NDMA = 8
ENGS = ("pe", "act", "dve", "pool", "sp")
C0 = -0.5 * math.log(128.0)
GELU_K = 1.5957691216057308


class Op:
    __slots__ = ("eng", "fn", "deps", "waited", "semval", "dma", "slot", "dval")


class Sched:
    def __init__(self):
        self.ops = {e: [] for e in ENGS}
        self.lastw = {}
        self.rd = {}
        self.dma_rr = {e: 0 for e in ENGS}
        self.dma_last = {}
        self.pending = {e: [] for e in ENGS}
        self.dmas_since = []

    def add(self, eng, fn, r=(), w=(), dma=False):
        o = Op()
        o.eng = eng; o.fn = fn; o.dma = dma; o.waited = False; o.deps = []
        o.semval = 0; o.slot = 0; o.dval = 0
        deps = []
        for k in r:
            lw = self.lastw.get(k)
            if lw is not None:
                deps.append(lw)
        for k in w:
            lw = self.lastw.get(k)
            if lw is not None:
                deps.append(lw)
            deps.extend(self.rd.get(k, {}).values())
        deps.extend(self.pending[eng])
        self.pending[eng] = []
        if dma:
            o.slot = self.dma_rr[eng] % NDMA
            self.dma_rr[eng] += 1
            prev = self.dma_last.get((eng, o.slot))
            if prev is not None:
                deps.append(prev)
            self.dma_last[(eng, o.slot)] = o
            self.dmas_since.append(o)
        seen = set()
        for d in deps:
            if d is o or id(d) in seen:
                continue
            seen.add(id(d))
            if (not d.dma) and d.eng == eng and eng == "pe":
                continue
            d.waited = True
            o.deps.append(d)
        for k in w:
            self.lastw[k] = o
            self.rd[k] = {}
        for k in r:
            self.rd.setdefault(k, {})[("d", id(o)) if dma else eng] = o
        self.ops[eng].append(o)
        return o

    def barrier(self):
        lasts = []
        for e in ENGS:
            for o in reversed(self.ops[e]):
                if not o.dma:
                    lasts.append(o)
                    break
        lasts.extend(self.dmas_since)
        self.dmas_since = []
        for e in ENGS:
            self.pending[e] = list(self.pending[e]) + lasts

    def finalize(self):
        for e in ENGS:
            c = 0
            for o in self.ops[e]:
                if (not o.dma) and o.waited:
                    c += 1
                    o.semval = c
        cnt = {}
        for e in ENGS:
            for o in self.ops[e]:
                if o.dma:
                    key = (e, o.slot)
                    cnt[key] = cnt.get(key, 0) + 16
                    o.dval = cnt[key]

    def emit(self, e, eng, sems, dsems):
        waited = {}
        for o in self.ops[e]:
            for d in o.deps:
                if d.dma:
                    key = ("d", d.eng, d.slot); s = dsems[(d.eng, d.slot)]; v = d.dval
                else:
                    key = ("c", d.eng); s = sems[d.eng]; v = d.semval
                if waited.get(key, 0) >= v:
                    continue
                eng.wait_ge(s, v)
                waited[key] = v
            ins = o.fn(eng)
            if o.dma:
                ins.then_inc(dsems[(e, o.slot)], 16)
            elif o.waited:
                ins.then_inc(sems[e], 1)


def build(debug=False):
    nc = bass.Bass("TRN2", target_bir_lowering=False)

    def dram(name, shape, kind="ExternalInput"):
        return nc.dram_tensor(name, shape, F32, kind=kind).ap()

    x_d = dram("x", [2048, 1024]); mem_d = dram("mem", [256, 1024])
    win_d = dram("w_in", [128, 8, 7184]); wkv_d = dram("w_kv", [128, 8, 1024])
    wbr_d = dram("w_br", [3, 128, 4, 1024]); wout_d = dram("w_out", [128, 8, 1024])
    wup_d = dram("w_up", [128, 8, 5632]); wdn_d = dram("w_dn", [128, 22, 1024])
    pvec_d = dram("pvec", [128, NPV]); fg_d = dram("fg", [128, 1024]); gb_d = dram("gb", [16, 1])
    idf_d = dram("ident", [128, 128]); tri_d = dram("tri", [128, 2, 128])
    sel_d = dram("sel", [16, 2, 8, 128]); cmb_d = dram("cmb", [16, 3, 16])
    nab_d = dram("nab", [128, 8, 9, 128])
    y_d = dram("y", [2048, 1024], kind="ExternalOutput")
    dbg_d = {}
    if debug:
        dbg_d["HT"] = nc.dram_tensor("dbg_HT", [128, 8, 2048], BF16, kind="ExternalOutput").ap()
        dbg_d["YT"] = nc.dram_tensor("dbg_YT", [128, 12, 2048], BF16, kind="ExternalOutput").ap()
        dbg_d["MT"] = nc.dram_tensor("dbg_MT", [128, 8, 2048], BF16, kind="ExternalOutput").ap()
        dbg_d["X1"] = nc.dram_tensor("dbg_X1", [128, 16, 1024], F32, kind="ExternalOutput").ap()

    Sd = Sched()
    from contextlib import ExitStack
    with ExitStack() as es:
        def sb(name, shape, dtype=F32):
            return es.enter_context(nc.sbuf_tensor("sb_" + name, shape, dtype))
        A1 = sb("A1", [128, 16384]); A2 = sb("A2", [128, 20480]); WA = sb("WA", [128, 6272])
        identb = sb("identb", [128, 128], BF16); identf = sb("identf", [128, 128])
        tri = sb("tri", [128, 2, 128]); sel = sb("sel", [16, 2, 8, 128]); cmb = sb("cmb", [16, 3, 16])
        pvec = sb("pvec", [128, NPV]); gb = sb("gb", [16, 1])
        NB = sb("NB", [128, 9216], BF16)
        hnst = sb("hnst", [128, 2, 1024], BF16)
        junk = sb("junk", [128, 1024], BF16)
        st = sb("st", [128, 256])
        PS = es.enter_context(nc.psum_tensor("PS", [128, 8, 512], F32))
        sems = {e: es.enter_context(nc.semaphore("s_" + e)) for e in ENGS}
        dsems = {(e, i): es.enter_context(nc.semaphore("d_%s%d" % (e, i))) for e in ("sp", "pool") for i in range(NDMA)}
        block = es.enter_context(nc.Block())

        A1b = A1[:, :].bitcast(BF16)
        A2b = A2[:, :].bitcast(BF16)
        WAb = WA[:, :].bitcast(BF16)
        HT = A2b[:, 0:16384].rearrange("p (a b) -> p a b", b=2048)
        YT = A2b[:, 16384:40960].rearrange("p (a b) -> p a b", b=2048)
        X = A1[:, :].rearrange("p (a b) -> p a b", b=1024)
        X1 = A2[:, 0:16384].rearrange("p (a b) -> p a b", b=1024)

        def psf(b0, nb):
            return PS[:, b0:b0 + nb, :].rearrange("p a b -> p (a b)")

        def psb16(b):
            return PS[:, b, :].bitcast(BF16)

        def PSK(*banks):
            return [("ps", b) for b in banks]

        def dma(q, out, in_, r=(), w=()):
            return Sd.add(q, lambda e, out=out, in_=in_: e.dma_start(out=out, in_=in_), r, w, dma=True)

        def pe_mms(lst, r, w, skip=False):
            def fn(e, lst=lst, skip=skip):
                ins = None
                for (o_, l_, r_, st_, sp_) in lst:
                    if skip:
                        ins = e.matmul(o_, l_, r_, start=st_, stop=sp_, skip_group_check=True)
                    else:
                        ins = e.matmul(o_, l_, r_, start=st_, stop=sp_)
                return ins
            return Sd.add("pe", fn, r, w)

        def pe_tr(lst, r, w):
            def fn(e, lst=lst):
                ins = None
                for (o_, i_) in lst:
                    ins = e.transpose(o_, i_, identb[:, :])
                return ins
            return Sd.add("pe", fn, r, w)

        def act(out, in_, func, r, w, bias=None, scale=None, accum=None):
            kw = {}
            if bias is not None: kw["bias"] = bias
            if scale is not None: kw["scale"] = scale
            if accum is not None: kw["accum_out"] = accum
            return Sd.add("act", lambda e, out=out, in_=in_, func=func, kw=kw: e.activation(out=out, in_=in_, func=func, **kw), r, w)

        def ts(eng, out, in0, s1, s2, op0, op1, r, w):
            if op1 is None:
                return Sd.add(eng, lambda e: e.tensor_scalar(out, in0, s1, None, op0), r, w)
            return Sd.add(eng, lambda e: e.tensor_scalar(out, in0, s1, s2, op0, op1), r, w)

        def stt(eng, out, in0, sc, in1, op0, op1, r, w):
            return Sd.add(eng, lambda e: e.scalar_tensor_tensor(out, in0, sc, in1, op0, op1), r, w)

        def tt(eng, out, in0, in1, op, r, w):
            return Sd.add(eng, lambda e: e.tensor_tensor(out, in0, in1, op), r, w)

        def rstd_ops(ssq, ms, rs, scale, r, w):
            ts("dve", ms, ssq, scale, 1e-6, ALU.mult, ALU.add, r, ["rs_tmp"])
            act(ms, ms, AF.Sqrt, [], ["rs_tmp"])
            Sd.add("dve", lambda e: e.reciprocal(rs, ms), ["rs_tmp"], w)

        dma("sp", pvec[:, :], pvec_d, w=["pvec"])
        dma("sp", identf[:, :], idf_d, w=["identf"])
        dma("sp", tri[:, :, :], tri_d, w=["tri"])
        dma("sp", sel[:, :, :, :], sel_d, w=["sel"])
        dma("sp", cmb[:, :, :], cmb_d, w=["cmb"])
        dma("sp", gb[:, :], gb_d, w=["gb"])
        dma("pool", identb[:, :], idf_d, w=["identb"])
        for t in range(NT):
            dma("sp", X[:, t, :], x_d[t * 128:(t + 1) * 128, :], w=[("X", t)])
        nab = NB[:, :].rearrange("p (a b c) -> p a b c", b=9, c=128)
        for h in range(8):
            dma("pool", nab[:, h, :, :], nab_d[:, h, :, :], r=[("X", NT - 1)], w=[("nab", h)])

        def norm_group(src, nt_list, gcol, runs, dst_off, srckey, dstkey, stat0, g0, slots=None, banks=(6, 7)):
            n = len(nt_list)
            ssq = st[:, stat0:stat0 + n]; ms = st[:, stat0 + 16:stat0 + 16 + n]; rs = st[:, stat0 + 32:stat0 + 32 + n]
            idx = list(range(g0, min(g0 + 4, n)))
            for i in idx:
                t = nt_list[i]
                act(junk[:, :], src[:, t, :], AF.Square, [(srckey, t)], [("ssq", stat0, i), "junk"], accum=ssq[:, i:i + 1])
            gs = slice(idx[0], idx[-1] + 1)
            tk = ("rs_tmp", stat0, g0)
            ts("dve", ms[:, gs], ssq[:, gs], 1.0 / 1024.0, 1e-6, ALU.mult, ALU.add, [("ssq", stat0, i) for i in idx], [tk])
            act(ms[:, gs], ms[:, gs], AF.Sqrt, [], [tk])
            Sd.add("dve", lambda e, gs=gs: e.reciprocal(rs[:, gs], ms[:, gs]), [tk], [("rs", stat0, g0)])
            if slots is None:
                slots = [hnst[:, 0, :], hnst[:, 1, :]]
            nsl = len(slots); nbk = len(banks)

            def scale_tile(i):
                t = nt_list[i]
                sl = i % nsl
                if i % 2 == 0:
                    act(slots[sl], src[:, t, :], AF.Copy, [(srckey, t), ("rs", stat0, g0)], [("hnst", sl)], scale=rs[:, i:i + 1])
                else:
                    ts("dve", slots[sl], src[:, t, :], rs[:, i:i + 1], None, ALU.mult, None,
                       [(srckey, t), ("rs", stat0, g0)], [("hnst", sl)])

            two_phase = nsl >= len(idx)
            if two_phase:
                for i in idx:
                    scale_tile(i)
            for i in idx:
                if not two_phase:
                    scale_tile(i)
                t = nt_list[i]
                sl = i % nsl
                bank = banks[i % nbk]
                pb = psb16(bank).rearrange("p (a b) -> p a b", b=128)
                pe_tr([(pb[:, kc, :], slots[sl][:, kc * 128:(kc + 1) * 128]) for kc in range(8)],
                      [("hnst", sl), "identb"], PSK(bank))
                c0 = dst_off + i * 128
                for (rap, k0, nk) in runs:
                    tt("dve", rap[:, :, c0:c0 + 128], pb[:, k0:k0 + nk, :],
                       pvec[:, gcol + k0:gcol + k0 + nk].unsqueeze(2).to_broadcast([128, nk, 128]), ALU.mult,
                       ["pvec"], PSK(bank) + [(dstkey, c0 // 512)])

        def norm_to_T(src, nt_list, gcol, dstT, dst_off, srckey, dstkey, stat0):
            for g0 in range(0, len(nt_list), 4):
                norm_group(src, nt_list, gcol, [(dstT, 0, 8)], dst_off, srckey, dstkey, stat0, g0)

        HS4a = [hnst[:, 0, :], hnst[:, 1, :], A2b[:, 16384:17408], A2b[:, 17408:18432]]
        for g0 in range(0, NT, 4):
            norm_group(X, list(range(NT)), 0, [(HT, 0, 8)], 0, "X", "HT", 0, g0, slots=HS4a, banks=(4, 5, 6, 7))
        HTK = [("HT", g) for g in range(4)]
        act(NB[:, :], NB[:, :], AF.Exp, [], [("nab", h) for h in range(8)])
        WAb_ = WA[:, :].bitcast(BF16)
        dma("pool", WAb_[:, 0:4096].rearrange("p (a b) -> p a b", b=512), win_d[:, :, 1024:1536], w=["Wv"])
        dma("pool", WAb_[:, 4096:4224].rearrange("p (a b) -> p a b", b=16), win_d[:, :, 2048:2064], w=["Wg"])
        Sd.barrier()

        def proj_fm(wlist, rhsT, nk, b0, rkeys, gkey=None):
            for g in range(4):
                rk = list(rkeys) + ([(gkey, g)] if gkey is not None else [])
                pe_mms([(PS[:, b0 + g, :], wlist[kc], rhsT[kc][:, g * 512:(g + 1) * 512], kc == 0, kc == nk - 1)
                        for kc in range(nk)], rk, PSK(b0 + g))

        AX = mybir.AxisListType
        Gb = A1[0:16, 0:2048]; SPt = A1[0:16, 2048:4096]; CS = A1[0:16, 4096:6144]
        ZR = A1[0:16, 6144:7168].bitcast(BF16)
        Btok = A1[:, 7168:7296].rearrange("p (a b) -> p a b", b=8)
        Ftok = A1[:, 7296:7424].rearrange("p (a b) -> p a b", b=8)
        BtokT = A1[:, 7168:7296].rearrange("p (t m) -> p m t", m=8)
        FtokT = A1[:, 7296:7424].rearrange("p (t m) -> p m t", m=8)
        EB = A1[:, 7424:7560].rearrange("p (a b) -> p a b", b=17)
        VEC = A1[:, 7560:8072].rearrange("p (c k j) -> p c k j", k=4, j=16)
        Vaug = A1b[:, 16144:24400].rearrange("p (a b c) -> p a b c", b=4, c=129)
        Vfl = A1b[:, 16144:24400].rearrange("p (a c) -> p a c", c=129)
        qTh = A1b[:, 24400:26448]; kTh = A1b[:, 26448:28496]
        sgO2 = [A1b[:, 28496:30544], A2b[:, 38912:40960]]
        kTok = A1b[:, 30544:32592].rearrange("p (a b) -> p a b", b=128)
        mst = A1[:, 16296:16384]
        OO = A1[:, 0:4128].rearrange("p (d j c) -> p d j c", d=2, c=129)
        OOf = A1[:, 0:4128].rearrange("p (a c) -> p a c", c=129)
        hnB = A2b[:, 36864:38912].rearrange("p (a b) -> p a b", b=128)
        Vb = [A1b[:, 8256:10320].rearrange("p (a c) -> p a c", c=129),
              WAb[:, 10368:12432].rearrange("p (a c) -> p a c", c=129)]
        PmT = A1b[:, 11336:12360].rearrange("p (d s c) -> p d s c", d=2, c=128)
        Cf = A1[:, 6180:6438].rearrange("p (d c) -> p d c", c=129)
        Cb = A1b[:, 12876:13656].rearrange("p (d s c) -> p d s c", d=2, c=130)
        CA = A2[:, 12288:14336]; HH = A2[:, 14336:16384]; SQ = A2[:, 16384:18432]
        HH3 = HH.rearrange("p (a b) -> p a b", b=128); SQ3 = SQ.rearrange("p (a b) -> p a b", b=128)
        Wv = WAb[:, 0:4096].rearrange("p (a b) -> p a b", b=512)
        Wg = WAb[:, 4096:4224].rearrange("p (a b) -> p a b", b=16)

        def Wqko(slot, j):
            o0 = 4224 + (slot * 3 + j) * 1024
            return WAb[:, o0:o0 + 1024].rearrange("p (a b) -> p a b", b=128)

        HTl = [HT[:, kc, :] for kc in range(8)]
        for g in range(4):
            pe_mms([(PS[0:16, g, :], Wg[:, kc, :], HT[:, kc, g * 512:(g + 1) * 512], kc == 0, kc == 7) for kc in range(8)],
                   ["Wg"] + HTK, PSK(g))
        P16 = PS[0:16, 0:4, :].rearrange("p a b -> p (a b)")
        act(Gb, P16, AF.Identity, ["gb"], PSK(0, 1, 2, 3) + ["Gb"], bias=gb[:, 0:1])
        Sd.add("pool", lambda e: e.memset(Vfl[:, :, 128:129], 1.0), [], ["Vones"])
        for t in range(NT):
            b = 6 + t % 2
            pe_mms([(PS[:, b, :], HT[:, kc, t * 128:(t + 1) * 128], Wv[:, kc, :], kc == 0, kc == 7) for kc in range(8)],
                   ["Wv"] + HTK, PSK(b))
            act(Vaug[:, t, :, 0:128], PS[:, b, :].rearrange("p (a b) -> p a b", b=128), AF.Copy, [], PSK(b) + [("V", t)])
        act(SPt, Gb, AF.Exp, ["Gb"], ["SPt"], scale=-1.0)
        act(SPt, SPt, AF.Ln, [], ["SPt"], bias=1.0)
        Sd.add("pool", lambda e: e.memset(ZR, 0.0), [], ["ZR"])
        Sd.add("dve", lambda e: e.tensor_tensor_scan(CS, SPt, ZR, 0.0, ALU.add, ALU.add), ["SPt", "ZR"], ["CS"])
        for t in range(NT):
            tsl = slice(t * 128, (t + 1) * 128)
            pe_mms([(PS[:, 4, t * 16:(t + 1) * 16], Gb[:, tsl], cmb[:, 0, :], True, False),
                    (PS[:, 4, t * 16:(t + 1) * 16], CS[:, tsl], cmb[:, 1, :], False, False),
                    (PS[:, 4, t * 16:(t + 1) * 16], SPt[:, tsl], cmb[:, 2, :], False, True)],
                   ["Gb", "CS", "SPt", "cmb"], PSK(4))
        p4 = PS[:, 4, 0:256].rearrange("p (t m) -> p t m", m=16)
        ts("dve", Btok, p4[:, :, 0:8], C0, None, ALU.add, None, [], PSK(4) + ["Btok"])
        Sd.add("dve", lambda e: e.tensor_copy(Ftok, p4[:, :, 8:16]), [], PSK(4) + ["Ftok"])
        ebl = []
        for hd in range(4):
            ebl.append((PS[:, 6, hd * 18 + 1:hd * 18 + 17], sel[:, 1, hd, :], CS[:, 127:2048:128], True, True))
        for hd in range(4):
            c = 4 + hd
            ebl.append((PS[:, 6, c * 18:c * 18 + 16], sel[:, 0, 4 + hd, :], CS[:, 0:2048:128], True, False))
            ebl.append((PS[:, 6, c * 18:c * 18 + 16], sel[:, 1, 4 + hd, :], SPt[:, 0:2048:128], False, True))
            ebl.append((PS[:, 6, c * 18 + 16:c * 18 + 18], sel[:, 0, 4 + hd, :], CS[:, 2046:2048], True, True))
        pe_mms(ebl, ["sel", "CS", "SPt"], PSK(6))
        pe6 = PS[:, 6, 0:144].rearrange("p (a b) -> p a b", b=18)
        Sd.add("pool", lambda e: e.memset(A1[:, 7424:7560], 0.0), [], ["EB"])
        Sd.add("dve", lambda e: e.tensor_copy(EB[:, 0:4, 1:17], pe6[:, 0:4, 1:17]), [], PSK(6) + ["EB"])
        Sd.add("dve", lambda e: e.tensor_copy(EB[:, 4:8, 0:16], pe6[:, 4:8, 0:16]), [], PSK(6) + ["EB"])
        Sd.add("dve", lambda e: e.tensor_copy(EB[:, 4:8, 16:17], pe6[:, 4:8, 17:18]), [], PSK(6) + ["EB"])
        vk = ["EB", "Btok", "Ftok"]
        tt("dve", VEC[:, 0:4, 0, :], FtokT[:, 0:4, :], EB[:, 0:4, 0:16], ALU.subtract, vk, [("VEC", 0)])
        tt("dve", VEC[:, 0:4, 1, :], BtokT[:, 0:4, :], EB[:, 0:4, 0:16], ALU.add, vk, [("VEC", 1)])
        tt("dve", VEC[:, 0:4, 2, :], BtokT[:, 0:4, :], EB[:, 0:4, 1:17], ALU.add, vk, [("VEC", 2)])
        tt("dve", VEC[:, 0:4, 3, :], EB[:, 0:4, 1:17], EB[:, 0:4, 0:16], ALU.subtract, vk, [("VEC", 3)])
        tt("dve", VEC[:, 4:8, 0, :], FtokT[:, 4:8, :], EB[:, 4:8, 1:17], ALU.subtract, vk, [("VEC", 4)])
        tt("dve", VEC[:, 4:8, 1, :], BtokT[:, 4:8, :], EB[:, 4:8, 1:17], ALU.add, vk, [("VEC", 5)])
        tt("dve", VEC[:, 4:8, 2, :], BtokT[:, 4:8, :], EB[:, 4:8, 0:16], ALU.add, vk, [("VEC", 6)])
        tt("dve", VEC[:, 4:8, 3, :], EB[:, 4:8, 0:16], EB[:, 4:8, 1:17], ALU.subtract, vk, [("VEC", 7)])
        act(A1[:, 7560:8072], A1[:, 7560:8072], AF.Exp, [], [("VEC", i) for i in range(8)])
        VK8 = [("VEC", i) for i in range(8)]
        VK = [("V", t) for t in range(NT)] + ["Vones"]
        dma("pool", WAb[:, 0:4096].rearrange("p (a b) -> p a b", b=512), win_d[:, :, 2064:2576], w=[("Wna", 0), "Wv"])
        for j, c0 in enumerate((0, 512, 1536)):
            dma("pool", Wqko(0, j), win_d[:, :, c0:c0 + 128], w=[("Wqko", 0, j)])

        def conv_silu(chunk, dst, dstkey, func, b0=0):
            w0 = pvec[:, 48 + chunk * 3:49 + chunk * 3]; w1 = pvec[:, 49 + chunk * 3:50 + chunk * 3]
            w2 = pvec[:, 50 + chunk * 3:51 + chunk * 3]; bb = pvec[:, 72 + chunk:73 + chunk]
            pf = psf(b0, 4)
            bk = PSK(b0, b0 + 1, b0 + 2, b0 + 3)
            act(CA, pf, AF.Identity, ["pvec"], bk + ["CA"], bias=bb, scale=w1)
            stt("dve", CA[:, 1:2048], pf[:, 0:2047], w0, CA[:, 1:2048], ALU.mult, ALU.add, ["pvec"], bk + ["CA"])
            stt("dve", CA[:, 0:2047], pf[:, 1:2048], w2, CA[:, 0:2047], ALU.mult, ALU.add, ["pvec"], bk + ["CA"])
            act(dst, CA, func, ["CA"], [dstkey])

        def ml_phase_b(hd, part):
            OK_ = [("OO", d, q_) for d in range(2) for q_ in range(4)]
            dnB = mst[:, 0:32]; rdnB = mst[:, 32:64]; hss = mst[:, 64:80]
            aT = VEC[:, hd:8:4, 0, :]
            dn3 = dnB.rearrange("p (d j) -> p d j", d=2); rdn3 = rdnB.rearrange("p (d j) -> p d j", d=2)
            if part == 0:
                act(dnB.unsqueeze(2), OOf[:, :, 128:129], AF.Abs, OK_, ["dnB"])
                tt("dve", dn3, dn3, aT, ALU.mult, VK8, ["dnB"])
                ts("dve", dnB, dnB, 1.0, None, ALU.max, None, [], ["dnB"])
                Sd.add("dve", lambda e: e.reciprocal(rdnB, dnB), ["dnB"], ["rdnB"])
                tt("dve", rdn3, rdn3, aT, ALU.mult, VK8, ["rdnB"])
                tt("dve", HH3, OO[:, 0, :, 0:128], rdnB[:, 0:16].unsqueeze(2).to_broadcast([128, 16, 128]), ALU.mult, OK_ + ["rdnB"], ["HH"])
                tt("dve", SQ3, OO[:, 1, :, 0:128], rdnB[:, 16:32].unsqueeze(2).to_broadcast([128, 16, 128]), ALU.mult, OK_ + ["rdnB"], ["SQ"])
                tt("pool", HH, HH, SQ, ALU.add, ["SQ"], ["HH"])
                tt("pool", SQ, HH, HH, ALU.mult, ["HH"], ["SQ"])
            elif part == 1:
                Sd.add("dve", lambda e: e.tensor_reduce(hss, SQ3, AX.X, ALU.add), ["SQ"], ["hss"])
                ts("dve", hss, hss, 1.0 / 128.0, 1e-6, ALU.mult, ALU.add, [], ["hss"])
                act(hss, hss, AF.Sqrt, [], ["hss"])
                Sd.add("dve", lambda e: e.reciprocal(hss, hss), [], ["hss"])
                tt("dve", hnB, HH3, hss.unsqueeze(2).to_broadcast([128, 16, 128]), ALU.mult, ["HH", "hss"], ["hnB"])
            else:
                for half in range(2):
                    pb = psb16(2 + half).rearrange("p (a b) -> p a b", b=128)
                    pe_tr([(pb[:, j, :], hnB[:, half * 8 + j, :]) for j in range(8)], ["hnB", "identb"], PSK(2 + half))
                    stt("dve", YT[:, hd, half * 1024:(half + 1) * 1024], psb16(2 + half), pvec[:, 168 + hd:169 + hd],
                        sgO2[hd % 2][:, half * 1024:(half + 1) * 1024], ALU.mult, ALU.mult, ["pvec", ("sgO", hd % 2)],
                        PSK(2 + half) + [("YT", hd, 2 * half), ("YT", hd, 2 * half + 1)])

        Wnav = WAb[:, 8192:12288].rearrange("p (a b) -> p a b", b=512)
        for hd in range(4):
            slot = hd % 2
            for j, c0 in enumerate((0, 512, 1536)):
                if hd > 0:
                    dma("pool", Wqko(slot, j), win_d[:, :, c0 + hd * 128:c0 + (hd + 1) * 128], w=[("Wqko", slot, j)])
            proj_fm([Wqko(slot, 0)[:, kc, :] for kc in range(8)], HTl, 8, 0, [("Wqko", slot, 0)] + HTK)
            proj_fm([Wqko(slot, 1)[:, kc, :] for kc in range(8)], HTl, 8, 4, [("Wqko", slot, 1)] + HTK)
            conv_silu(hd, qTh, "qTh", AF.Silu)
            conv_silu(4 + hd, kTh, "kTh", AF.Silu, b0=4)
            proj_fm([Wqko(slot, 2)[:, kc, :] for kc in range(8)], HTl, 8, 0, [("Wqko", slot, 2)] + HTK)
            act(sgO2[hd % 2], psf(0, 4), AF.Sigmoid, [], PSK(0, 1, 2, 3) + [("sgO", hd % 2)])
            if hd == 3:
                S1K = [("Wqko", 1, 0), ("Wqko", 1, 1), ("Wqko", 1, 2)]
                dma("pool", Wnav[:, 0:4, :], win_d[:, 0:4, 3088:3600], w=[("Wna", 2, 0)] + S1K)
                dma("pool", WAb[:, 4096:8192].rearrange("p (a b) -> p a b", b=512), win_d[:, :, 2576:3088],
                    w=[("Wna", 1), "Wg", ("Wqko", 0, 0), ("Wqko", 0, 1), ("Wqko", 0, 2), ("Wqko", 1, 0)])
            if hd > 0:
                ml_phase_b(hd - 1, 0)
            else:
                Sd.barrier()
            for half in range(2):
                pb = psb16(2 + half).rearrange("p (a b) -> p a b", b=128)
                pe_tr([(pb[:, j, :], kTh[:, (half * 8 + j) * 128:(half * 8 + j + 1) * 128]) for j in range(8)], ["kTh", "identb"], PSK(2 + half))
                act(kTok[:, half * 8:(half + 1) * 8, :], pb[:, :, :], AF.Copy, [], PSK(2 + half) + [("kTok", half)])
            KTK = [("kTok", 0), ("kTok", 1)]
            if hd == 0:
                for dr in range(2):
                    c = dr * 4 + hd
                    tt("dve", Vb[dr], Vaug[:, :, hd, :], VEC[:, c, 1, :].unsqueeze(2).to_broadcast([128, 16, 129]), ALU.mult,
                       VK + VK8, [("Vb", dr)])
            Sd.add("pool", lambda e: e.memset(A1[:, 6180:6438], 0.0), [], [("Cf", 0), ("Cf", 1)])
            Sd.add("pool", lambda e: e.memset(A1b[:, 12876:13656], 0.0), [], [("Cb", d, p_) for d in range(2) for p_ in range(3)])
            def ml_a(step):
                par = step % 2
                bX = 4 + par; bY = 6 + par
                Js = (step, NT - 1 - step)
                lst = []
                for dr in range(2):
                    J = Js[dr]; Jsl = slice(J * 128, (J + 1) * 128)
                    lst.append((PS[:, bX, dr * 128:(dr + 1) * 128], kTh[:, Jsl], qTh[:, Jsl], True, True))
                pe_mms(lst, ["kTh", "qTh"], PSK(bX))
                pe_mms([(PS[:, bX, 256:385], kTok[:, Js[0], :], Vb[0][:, Js[0], :], True, True)], [("Vb", 0)] + KTK, PSK(bX))
                pe_mms([(PS[:, bY, 0:129], kTok[:, Js[1], :], Vb[1][:, Js[1], :], True, True)], [("Vb", 1)] + KTK, PSK(bY))
                tt("dve", PmT[:, :, par, :], PS[:, bX, 0:256].rearrange("p (a b) -> p a b", b=128), tri[:, :, :], ALU.mult,
                   ["tri"], PSK(bX) + [("PmT", 0, par), ("PmT", 1, par)])
                if step < NT - 1:
                    for dr in range(2):
                        J = Js[dr]; c = dr * 4 + hd
                        Jp = J - 1 if dr == 0 else J + 1
                        gp = VEC[:, c, 3, Jp:Jp + 1] if step > 0 else VEC[:, c, 3, J:J + 1]
                        usrc = PS[:, bX, 256:385] if dr == 0 else PS[:, bY, 0:129]
                        stt("dve", Cf[:, dr, :], Cf[:, dr, :], gp, usrc, ALU.mult, ALU.add,
                            VK8, PSK(bX if dr == 0 else bY) + [("Cf", dr)])
                        act(Cb[:, dr, (step + 1) % 3, 0:129], Cf[:, dr, :], AF.Copy, [("Cf", dr)] + VK8, [("Cb", dr, (step + 1) % 3)],
                            scale=VEC[:, c, 3, J:J + 1])

            def ml_b(step):
                par = step % 2
                bY = 6 + par
                Js = (step, NT - 1 - step)
                lst = []
                for dr in range(2):
                    J = Js[dr]; Jsl = slice(J * 128, (J + 1) * 128)
                    oc = 129 + dr * 129
                    lst.append((PS[:, bY, oc:oc + 129], PmT[:, dr, par, :], Vb[dr][:, J, :], True, False))
                    lst.append((PS[:, bY, oc:oc + 129], qTh[:, Jsl], Cb[:, dr, step % 3, 0:129], False, True))
                pe_mms(lst, [("PmT", 0, par), ("PmT", 1, par), "qTh", ("Cb", 0, step % 3), ("Cb", 1, step % 3), ("Vb", 0), ("Vb", 1)], PSK(bY))
                act(OOf[:, step:32 - step:31 - 2 * step, :], PS[:, bY, 129:387].rearrange("p (a b) -> p a b", b=129), AF.Copy, [],
                    PSK(bY) + [("OO", 0, Js[0] // 4), ("OO", 1, Js[1] // 4)])

            ml_a(0)
            for step in range(NT):
                if step + 1 < NT:
                    ml_a(step + 1)
                ml_b(step)
                if hd > 0 and step == 4:
                    ml_phase_b(hd - 1, 1)
                if hd > 0 and step == 9:
                    ml_phase_b(hd - 1, 2)
            if hd < 3:
                for dr in range(2):
                    c = dr * 4 + hd + 1
                    tt("dve", Vb[dr], Vaug[:, :, hd + 1, :], VEC[:, c, 1, :].unsqueeze(2).to_broadcast([128, 16, 129]), ALU.mult,
                       VK + VK8, [("Vb", dr)])
        Vna = A1b[:, 16384:24704].rearrange("p (a b c) -> p a b c", b=8, c=65)
        Vnf = A1b[:, 16384:24704].rearrange("p (a c) -> p a c", c=65)
        dma("pool", Wnav[:, 4:8, :], win_d[:, 4:8, 3088:3600], w=[("Wna", 2, 1), ("Wqko", 1, 2), ("Vb", 1)])
        VALIAS = VK + ["qTh"]
        Sd.add("pool", lambda e: e.memset(Vnf[:, :, 64:65], 1.0), [], ["Vnones"] + VALIAS)
        ml_phase_b(3, 0)
        for t in range(NT):
            if t == 5:
                ml_phase_b(3, 1)
            if t == 10:
                ml_phase_b(3, 2)
            b = 4 + t % 2
            pe_mms([(PS[:, b, :], HT[:, kc, t * 128:(t + 1) * 128], Wnav[:, kc, :], kc == 0, kc == 7) for kc in range(8)],
                   [("Wna", 2, 0), ("Wna", 2, 1)] + HTK, PSK(b))
            act(Vna[:, t, :, 0:64], PS[:, b, :].rearrange("p (a b) -> p a b", b=64), AF.Copy, [], PSK(b) + [("Vn", t)] + VALIAS)
        Sd.barrier()

        qTn = A1b[:, 0:8192].rearrange("p (a b) -> p a b", b=2048)
        kTn = A1b[:, 8192:16384].rearrange("p (a b) -> p a b", b=2048)
        Vna = A1b[:, 16384:24704].rearrange("p (a b c) -> p a b c", b=8, c=65)
        Vnf = A1b[:, 16384:24704].rearrange("p (a c) -> p a c", c=65)
        pTn = A1b[:, 24704:27264].rearrange("p (s h b) -> p s h b", s=2, h=2)
        pEx = A1b[:, 27264:29824].rearrange("p (s h b) -> p s h b", s=2, h=2)
        onb = A1b[:, 29824:30336]
        rdna = A1[:, 15168:15176]
        Wna3 = [WAb[:, j * 4096:(j + 1) * 4096].rearrange("p (a b) -> p a b", b=512) for j in range(3)]
        for c in range(4):
            proj_fm([Wna3[0][:, kc, c * 128:(c + 1) * 128] for kc in range(8)], HTl, 8, 4 * (c % 2), [("Wna", 0)] + HTK)
            b0 = 4 * (c % 2)
            act(qTn[:, c, :], psf(b0, 4), AF.Copy, [], PSK(b0, b0 + 1, b0 + 2, b0 + 3) + [("qTn", c)], scale=0.125)
        for c in range(4):
            b0 = 4 * (c % 2)
            proj_fm([Wna3[1][:, kc, c * 128:(c + 1) * 128] for kc in range(8)], HTl, 8, b0, [("Wna", 1)] + HTK)
            Sd.add("dve", lambda e, c=c, b0=b0: e.tensor_copy(kTn[:, c, :], psf(b0, 4)), [], PSK(b0, b0 + 1, b0 + 2, b0 + 3) + [("kTn", c)])
        NABK = [("nab", h) for h in range(8)]
        VNK = [("Vn", t) for t in range(NT)] + ["Vnones"]

        def na_keys(t):
            if t == 0:
                return [0, 1, 2, 3], [(4, 2), (7, 2)]
            if t == 1:
                return [0, 1, 2, 3], [(3, 3), (7, 1)]
            if t == 14:
                return [12, 13, 14, 15], [(1, 1), (3, 3)]
            if t == 15:
                return [12, 13, 14, 15], [(0, 2), (3, 2)]
            return [t - 2, t - 1, t, t + 1, t + 2], [(2, 5)]

        def na_sc(sl, hp, i):
            col = hp * 640 + i * 128
            return PS[:, 3 * sl + col // 512, col % 512:col % 512 + 128]

        def na_qk(t, c):
            tsl = slice(t * 128, (t + 1) * 128)
            kts, runs = na_keys(t)
            n = len(kts)
            sl = c % 2
            bk = PSK(3 * sl, 3 * sl + 1, 3 * sl + 2)
            lst = []
            for i, kt in enumerate(kts):
                for hp in range(2):
                    lst.append((na_sc(sl, hp, i), kTn[hp * 64:(hp + 1) * 64, c, kt * 128:(kt + 1) * 128],
                                qTn[hp * 64:(hp + 1) * 64, c, tsl], True, True))
            pe_mms(lst, [("qTn", c), ("kTn", c)], bk)

        def na_ew(t, c, hps):
            kts, runs = na_keys(t)
            n = len(kts)
            sl = c % 2
            bk = PSK(3 * sl, 3 * sl + 1, 3 * sl + 2)
            pin = psf(3 * sl, 3)[:, 0:1280].rearrange("p (h b) -> p h b", h=2)
            for hp in hps:
                act(pEx[:, sl, hp, 0:n * 128], pin[:, hp, 0:n * 128], AF.Exp, [], bk + [("pEx", sl, hp)])
                off = 0
                for (e0, m) in runs:
                    tt("dve", pTn[:, sl, hp, off * 128:(off + m) * 128], pEx[:, sl, hp, off * 128:(off + m) * 128],
                       nab[:, 2 * c + hp, e0:e0 + m, :].rearrange("p e k -> p (e k)"), ALU.mult,
                       [("pEx", sl, hp)] + NABK, [("pTn", sl, hp)])
                    off += m

        def na_out(h):
            return (6, h * 65) if h < 7 else (7, 0)

        def na_pv(t, c, hp):
            tsl = slice(t * 128, (t + 1) * 128)
            kts, runs = na_keys(t)
            n = len(kts)
            sl = c % 2
            h = 2 * c + hp
            ob, oc = na_out(h)
            pe_mms([(PS[:, ob, oc:oc + 65], pTn[:, sl, hp, i * 128:(i + 1) * 128], Vna[:, kt, h, :], i == 0, i == n - 1)
                    for i, kt in enumerate(kts)], [("pTn", sl, hp)] + VNK, PSK(ob))
            if h < 7:
                return
            pv6 = PS[:, 6, 0:455].rearrange("p (a b) -> p a b", b=65)
            Sd.add("dve", lambda e, pv6=pv6: e.reciprocal(rdna[:, 0:7].unsqueeze(2), pv6[:, :, 64:65]), [], PSK(6) + [("rdna", 0)])
            Sd.add("dve", lambda e: e.reciprocal(rdna[:, 7:8], PS[:, 7, 64:65]), [], PSK(7) + [("rdna", 1)])
            tt("dve", onb[:, 0:448].rearrange("p (a b) -> p a b", b=64), pv6[:, :, 0:64],
               rdna[:, 0:7].unsqueeze(2).to_broadcast([128, 7, 64]), ALU.mult, [("rdna", 0)], PSK(6) + [("onb", 0)])
            ts("dve", onb[:, 448:512], PS[:, 7, 0:64], rdna[:, 7:8], None, ALU.mult, None, [("rdna", 1)], PSK(7) + [("onb", 1)])
            pb = psb16(7)[:, 512:1024].rearrange("p (a b) -> p a b", b=128)
            pe_tr([(pb[:, cc, :], onb[:, cc * 128:(cc + 1) * 128]) for cc in range(4)], [("onb", 0), ("onb", 1), "identb"], PSK(7))
            Sd.add("act", lambda e, pb=pb, tsl=tsl: e.activation(out=YT[:, 4:8, tsl], in_=pb[:, 0:4, :], func=AF.Copy), [],
                   PSK(7) + [("YT", 4 + cc, t // 4) for cc in range(4)])

        Wxq = WAb[:, 0:4096].rearrange("p (a b) -> p a b", b=512)
        Wkv = WAb[:, 4096:12288].rearrange("p (a b) -> p a b", b=1024)
        WNAK = [("Wna", 0), ("Wna", 1), ("Wna", 2, 0), ("Wna", 2, 1)]
        dma("pool", Wxq, win_d[:, :, 3600:4112], w=["Wxq"] + WNAK)
        for j in range(2):
            dma("pool", Wkv[:, :, j * 512:(j + 1) * 512], wkv_d[:, :, j * 512:(j + 1) * 512], w=[("Wkv", j)] + WNAK)
        items = [(t, c) for t in range(NT) for c in range(4)]
        for i0_ in (0, 1):
            na_qk(*items[i0_]); na_ew(items[i0_][0], items[i0_][1], (0, 1))
        for i in range(len(items)):
            na_pv(items[i][0], items[i][1], 0)
            if i + 2 < len(items):
                na_qk(*items[i + 2]); na_ew(items[i + 2][0], items[i + 2][1], (0,))
            na_pv(items[i][0], items[i][1], 1)
            if i + 2 < len(items):
                na_ew(items[i + 2][0], items[i + 2][1], (1,))
        fg = NB[:, 0:2048].bitcast(F32)
        Sd.barrier()
        dma("sp", fg, fg_d, w=["fg"])

        qTx = A1b[:, 0:8192].rearrange("p (a b) -> p a b", b=2048)
        memst = A1[:, 4096:6144].rearrange("p (a b) -> p a b", b=1024)
        MHT = A1b[:, 14336:16384].rearrange("p (a b) -> p a b", b=256)
        KxT = A1b[:, 16384:17408].rearrange("p (a b) -> p a b", b=256)
        Vx = A1b[:, 17408:18440].rearrange("p (a b c) -> p a b c", b=4, c=129)
        Vxf = A1b[:, 17408:18440].rearrange("p (a c) -> p a c", c=129)
        pTx = A1b[:, 19472:21520].rearrange("p (a b) -> p a b", b=1024)
        oxb = A1b[:, 18952:19464]
        rdx = A1[:, 9732:9736]
        for mt in range(2):
            dma("sp", memst[:, mt, :], mem_d[mt * 128:(mt + 1) * 128, :], w=[("memst", mt)])
        norm_to_T(memst, [0, 1], 16, MHT, 0, "memst", "MHT", 80)
        MK = [("MHT", 0)]
        for hd in range(4):
            b = hd // 2
            pe_mms([(PS[:, b, (hd % 2) * 256:(hd % 2 + 1) * 256], Wkv[:, kc, hd * 128:(hd + 1) * 128], MHT[:, kc, :], kc == 0, kc == 7)
                    for kc in range(8)], [("Wkv", 0)] + MK, PSK(b))
        act(A1b[:, 16384:17408], psf(0, 2), AF.Copy, [], PSK(0, 1) + ["KxT"], scale=float(128.0 ** -0.5))
        Sd.add("pool", lambda e: e.memset(Vxf[:, :, 128:129], 1.0), [], ["Vxones"])
        for mt in range(2):
            pe_mms([(PS[:, 2 + mt, :], MHT[:, kc, mt * 128:(mt + 1) * 128], Wkv[:, kc, 512:1024], kc == 0, kc == 7)
                    for kc in range(8)], [("Wkv", 1)] + MK, PSK(2 + mt))
            act(Vx[:, mt, :, 0:128], PS[:, 2 + mt, :].rearrange("p (a b) -> p a b", b=128), AF.Copy, [], PSK(2 + mt) + [("Vx", mt)])
        for hd in range(4):
            b0 = 4 * (hd % 2)
            proj_fm([Wxq[:, kc, hd * 128:(hd + 1) * 128] for kc in range(8)], HTl, 8, b0, ["Wxq"] + HTK)
            if hd % 2 == 0:
                act(qTx[:, hd, :], psf(b0, 4), AF.Copy, [], PSK(b0, b0 + 1, b0 + 2, b0 + 3) + [("qTx", hd)])
            else:
                Sd.add("dve", lambda e, hd=hd, b0=b0: e.tensor_copy(qTx[:, hd, :], psf(b0, 4)), [], PSK(b0, b0 + 1, b0 + 2, b0 + 3) + [("qTx", hd)])
        VXK = [("Vx", 0), ("Vx", 1), "Vxones"]

        def xa_qk(t):
            tsl = slice(t * 128, (t + 1) * 128)
            sl = t % 2
            sb_ = 2 * sl
            pf2 = psf(sb_, 2)
            pe_mms([(pf2[:, hd * 256 + mt * 128:hd * 256 + (mt + 1) * 128], KxT[:, hd, mt * 128:(mt + 1) * 128], qTx[:, hd, tsl], True, True)
                    for hd in range(4) for mt in range(2)], [("qTx", hd) for hd in range(4)] + ["KxT"], PSK(sb_, sb_ + 1))
            act(pTx[:, sl, :], pf2, AF.Exp, [], PSK(sb_, sb_ + 1) + [("pTx", sl)])

        def xa_pv(t):
            tsl = slice(t * 128, (t + 1) * 128)
            sl = t % 2
            for half in range(2):
                lst = []
                for hd in (2 * half, 2 * half + 1):
                    oc = (hd % 2) * 129
                    for mt in range(2):
                        lst.append((PS[:, 4 + half, oc:oc + 129], pTx[:, sl, hd * 256 + mt * 128:hd * 256 + (mt + 1) * 128],
                                    Vx[:, mt, hd, :], mt == 0, mt == 1))
                pe_mms(lst, [("pTx", sl)] + VXK, PSK(4 + half))

        def xa_epi(t):
            tsl = slice(t * 128, (t + 1) * 128)
            for half in range(2):
                pv = PS[:, 4 + half, 0:258].rearrange("p (a b) -> p a b", b=129)
                rv = rdx[:, half * 2:(half + 1) * 2].unsqueeze(2)
                Sd.add("dve", lambda e, rv=rv, pv=pv: e.reciprocal(rv, pv[:, :, 128:129]), [], PSK(4 + half) + [("rdx", half)])
                tt("dve", oxb[:, half * 256:(half + 1) * 256].rearrange("p (a b) -> p a b", b=128), pv[:, :, 0:128],
                   rdx[:, half * 2:(half + 1) * 2].unsqueeze(2).to_broadcast([128, 2, 128]), ALU.mult,
                   [("rdx", half)], PSK(4 + half) + [("oxb", half)])
            tb = 6 + t % 2
            pb = psb16(tb).rearrange("p (a b) -> p a b", b=128)
            pe_tr([(pb[:, c, :], oxb[:, c * 128:(c + 1) * 128]) for c in range(4)], [("oxb", 0), ("oxb", 1), "identb"], PSK(tb))
            Sd.add("act", lambda e, pb=pb, tsl=tsl: e.activation(out=YT[:, 8:12, tsl], in_=pb[:, 0:4, :], func=AF.Copy), [],
                   PSK(tb) + [("YT", 8 + c, t // 4) for c in range(4)])

        XWK = ["Wxq", ("Wkv", 0), ("Wkv", 1)]
        for br in range(3):
            c0 = 4112 + br * 1024
            dma("pool", WAb[:, br * 1024:(br + 1) * 1024].rearrange("p (a b) -> p a b", b=128), win_d[:, :, c0:c0 + 128],
                w=[("Wmg", 0, br)] + XWK)
            dma("pool", WAb[:, 3072 + br * 512:3072 + (br + 1) * 512].rearrange("p (a b) -> p a b", b=128), wbr_d[br, :, :, 0:128],
                w=[("Wmb", 0, br)] + XWK)
        xa_qk(0); xa_qk(1)
        for t in range(NT):
            xa_pv(t)
            if t + 2 < NT:
                xa_qk(t + 2)
            xa_epi(t)
        Sd.barrier()

        MT = A1b[:, 0:16384].rearrange("p (a b) -> p a b", b=2048)
        SG = A1[:, 8192:10240]; ACm = A1[:, 10240:12288]

        def Wmg(slot, br):
            o0 = slot * 4608 + br * 1024
            return WAb[:, o0:o0 + 1024].rearrange("p (a b) -> p a b", b=128)

        def Wmb(slot, br):
            o0 = slot * 4608 + 3072 + br * 512
            return WAb[:, o0:o0 + 512].rearrange("p (a b) -> p a b", b=128)

        Wo = A1b[:, 24576:32768].rearrange("p (a b) -> p a b", b=1024)
        for j in range(2):
            dma("pool", Wo[:, :, j * 512:(j + 1) * 512], wout_d[:, :, j * 512:(j + 1) * 512], w=[("Wo", j)])
        for f in range(8):
            slot = f % 2
            for br in range(3):
                if f == 0:
                    continue
                c0 = 4112 + br * 1024 + f * 128
                dma("pool", Wmg(slot, br), win_d[:, :, c0:c0 + 128], w=[("Wmg", slot, br)])
                dma("pool", Wmb(slot, br), wbr_d[br, :, :, f * 128:(f + 1) * 128], w=[("Wmb", slot, br)])
            for br in range(3):
                proj_fm([Wmg(slot, br)[:, kc, :] for kc in range(8)], HTl, 8, 0, [("Wmg", slot, br)] + HTK)
                act(SG, psf(0, 4), AF.Sigmoid, ["pvec"], PSK(0, 1, 2, 3) + ["SG"], bias=pvec[:, 24 + br * 8 + f:25 + br * 8 + f])
                proj_fm([Wmb(slot, br)[:, kc, :] for kc in range(4)], [YT[:, br * 4 + kc, :] for kc in range(4)], 4, 4,
                        [("Wmb", slot, br)] + [("YT", br * 4 + kc, g) for kc in range(4) for g in range(4)])
                if br == 0:
                    tt("dve", ACm, psf(4, 4), SG, ALU.mult, ["SG"], PSK(4, 5, 6, 7) + ["ACm"])
                else:
                    tt("dve", SG, psf(4, 4), SG, ALU.mult, [], PSK(4, 5, 6, 7) + ["SG"])
                    if br == 1:
                        tt("dve", ACm, ACm, SG, ALU.add, ["SG"], ["ACm"])
                    else:
                        tt("dve", MT[:, f, :], ACm, SG, ALU.add, ["SG", "ACm"], [("MT", f)])
        if debug:
            dma("sp", dbg_d["HT"], HT, r=HTK)
            dma("sp", dbg_d["YT"], YT, r=[("YT", a, g) for a in range(12) for g in range(4)])
            dma("sp", dbg_d["MT"], MT, r=[("MT", f) for f in range(8)])
        Sd.barrier()

        xs = A2[:, 16384:18432].rearrange("p (a b) -> p a b", b=1024)
        H2runs = [(A1b[:, 16384:24576].rearrange("p (a b) -> p a b", b=2048), 0, 4),
                  (A2b[:, 36864:40960].rearrange("p (a b) -> p a b", b=2048), 4, 2),
                  (NB[:, 2048:6144].rearrange("p (a b) -> p a b", b=2048), 6, 2)]
        H2l = [A1b[:, 16384 + k * 2048:16384 + (k + 1) * 2048] for k in range(4)] + \
              [A2b[:, 36864 + k * 2048:36864 + (k + 1) * 2048] for k in range(2)] + \
              [NB[:, 2048 + k * 2048:2048 + (k + 1) * 2048] for k in range(2)]
        MTK = [("MT", f) for f in range(8)]
        HS4 = [hnst[:, 0, :], hnst[:, 1, :], NB[:, 6144:7168], NB[:, 7168:8192]]
        for t in range(NT):
            tsl = slice(t * 128, (t + 1) * 128)
            sl = t % 2
            dma("sp", xs[:, sl, :], x_d[t * 128:(t + 1) * 128, :], w=[("xs", sl)])
            b0 = 2 * sl
            for n in range(2):
                pe_mms([(PS[:, b0 + n, :], MT[:, f, tsl], Wo[:, f, n * 512:(n + 1) * 512], f == 0, f == 7) for f in range(8)],
                       MTK + [("Wo", n)], PSK(b0 + n))
            tt("dve", X1[:, t, :], psf(b0, 2), xs[:, sl, :], ALU.add, [("xs", sl)], PSK(b0, b0 + 1) + [("X1", t)])
            if t % 4 == 3 and t >= 7:
                norm_group(X1, list(range(NT)), 8, H2runs, 0, "X1", "H2T", 112, t - 7, slots=HS4, banks=(4, 5, 6, 7))
        norm_group(X1, list(range(NT)), 8, H2runs, 0, "X1", "H2T", 112, 12, slots=HS4, banks=(4, 5, 6, 7))
        if debug:
            dma("sp", dbg_d["X1"], X1, r=[("X1", t) for t in range(NT)])
        dma("pool", WAb[:, 0:1024].rearrange("p (a b) -> p a b", b=128), wup_d[:, :, 0:128], w=[("Wau", 0, 0)])
        dma("pool", WAb[:, 1024:2048].rearrange("p (a b) -> p a b", b=128), wup_d[:, :, 2816:2944], w=[("Wau", 0, 1)])
        dma("pool", WAb[:, 4096:5120], wdn_d[:, 0, :], w=[("Wd", 0)])

        ATl = [A1b[:, i * 2048:(i + 1) * 2048] for i in range(8)]
        CA2 = A2[:, 16384:18432]; CB2 = A1[:, 12288:14336]
        CA2K = ["CA2", ("xs", 0), ("xs", 1)]; CB2K = ["CB2", ("Wo", 0), ("Wo", 1)]
        H2K = [("H2T", g) for g in range(4)]

        def Wau(slot, j):
            o0 = (slot * 2 + j) * 1024
            return WAb[:, o0:o0 + 1024].rearrange("p (a b) -> p a b", b=128)

        def Wd(i):
            return WAb[:, 4096 + i * 1024:4096 + (i + 1) * 1024]

        ssq = st[:, 160:176]; ms = st[:, 176:192]; rs = st[:, 192:208]
        GF = 2

        def final_group(g0):
            gs = slice(g0, g0 + GF)
            for t in range(g0, g0 + GF):
                act(junk[:, :], X1[:, t, :], AF.Square, [("X1", t)], [("ssq3", t), "junk"], accum=ssq[:, t:t + 1])
            ts("dve", ms[:, gs], ssq[:, gs], 1.0 / 1024.0, 1e-6, ALU.mult, ALU.add, [("ssq3", t) for t in range(g0, g0 + GF)], [("rs3t", g0)])
            act(ms[:, gs], ms[:, gs], AF.Sqrt, [], [("rs3t", g0)])
            Sd.add("dve", lambda e, gs=gs: e.reciprocal(rs[:, gs], ms[:, gs]), [("rs3t", g0)], [("rs3", g0)])
            for t in range(g0, g0 + GF):
                if t % 3 != 2 or t >= NT - 2:
                    stt("dve", X1[:, t, :], X1[:, t, :], rs[:, t:t + 1], fg, ALU.mult, ALU.mult, [("rs3", g0), "fg"], [("X1", t)])
                else:
                    act(X1[:, t, :], X1[:, t, :], AF.Copy, [("rs3", g0)], [("X1", t)], scale=rs[:, t:t + 1])
                    tt("pool", X1[:, t, :], X1[:, t, :], fg, ALU.mult, ["fg"], [("X1", t)])
                dma("sp", y_d[t * 128:(t + 1) * 128, :], X1[:, t, :], r=[("X1", t)])

        groups = [list(range(0, 8)), list(range(8, 16)), list(range(16, 22))]
        ucount = 0
        for clist in groups:
            for i, c in enumerate(clist):
                slot = ucount % 2; ucount += 1
                if c > 0:
                    dma("pool", Wau(slot, 0), wup_d[:, :, c * 128:(c + 1) * 128], w=[("Wau", slot, 0)])
                    dma("pool", Wau(slot, 1), wup_d[:, :, 2816 + c * 128:2816 + (c + 1) * 128], w=[("Wau", slot, 1)])
                    dma("pool", Wd(i), wdn_d[:, c, :], w=[("Wd", i)])
                proj_fm([Wau(slot, 0)[:, kc, :] for kc in range(8)], H2l, 8, 0, [("Wau", slot, 0)], gkey="H2T")
                proj_fm([Wau(slot, 1)[:, kc, :] for kc in range(8)], H2l, 8, 4, [("Wau", slot, 1)], gkey="H2T")
                w0 = pvec[:, 80 + c * 3:81 + c * 3]; w1 = pvec[:, 81 + c * 3:82 + c * 3]; w2 = pvec[:, 82 + c * 3:83 + c * 3]
                bb = pvec[:, 146 + c:147 + c]
                pf = psf(0, 4)
                act(CA2, pf, AF.Identity, ["pvec"], PSK(0, 1, 2, 3) + CA2K, bias=bb, scale=w1)
                stt("dve", CA2[:, 1:2048], pf[:, 0:2047], w0, CA2[:, 1:2048], ALU.mult, ALU.add, ["pvec"], PSK(0, 1, 2, 3) + CA2K)
                stt("dve", CA2[:, 0:2047], pf[:, 1:2048], w2, CA2[:, 0:2047], ALU.mult, ALU.add, ["pvec"], PSK(0, 1, 2, 3) + CA2K)
                act(CB2, CA2, AF.Gelu_apprx_tanh, CA2K, CB2K)
                tt("dve", ATl[i], CB2, psf(4, 4), ALU.mult, CB2K, PSK(4, 5, 6, 7) + [("MT", i)])
            nk = len(clist)
            for t in range(NT):
                tsl = slice(t * 128, (t + 1) * 128)
                b0 = 2 * (t % 2)
                for n in range(2):
                    pe_mms([(PS[:, b0 + n, :], ATl[i][:, tsl], Wd(i)[:, n * 512:(n + 1) * 512], i == 0, i == nk - 1) for i in range(nk)],
                           [("MT", i) for i in range(nk)] + [("Wd", i) for i in range(nk)], PSK(b0 + n))
                tt("dve", X1[:, t, :], psf(b0, 2), X1[:, t, :], ALU.add, [], PSK(b0, b0 + 1) + [("X1", t)])
                if clist is groups[-1] and t % GF == GF - 1:
                    final_group(t - GF + 1)

        Sd.barrier()
        Sd.add("sp", lambda e: e.nop(), [], [])

        Sd.finalize()

        @block.tensor
        def _(e):
            Sd.emit("pe", e, sems, dsems)

        @block.scalar
        def _(e):
            Sd.emit("act", e, sems, dsems)

        @block.vector
        def _(e):
            Sd.emit("dve", e, sems, dsems)

        @block.gpsimd
        def _(e):
            Sd.emit("pool", e, sems, dsems)

        @block.sync
        def _(e):
            Sd.emit("sp", e, sems, dsems)
    return nc


def _nab_table(rpb):
    kk = np.arange(128); qq = np.arange(128)
    kl = kk // 64; kc = kk % 64; ql = qq // 64; qc = qq % 64
    col_start = np.clip(qc - 8, 0, 48)
    col_ok = (kc[:, None] >= col_start[None, :]) & (kc[:, None] < col_start[None, :] + 16)
    dc = np.clip(kc[:, None] - qc[None, :] + 15, 0, 30)
    out = np.empty((128, 8, 9, 128), np.float32)
    for e, d2 in enumerate([-3, -2, -2, -1, 0, 1, 2, 2, 3]):
        dr = 2 * d2 + kl[:, None] - ql[None, :] + 7
        ok = col_ok & (dr >= 0) & (dr <= 14)
        if e == 2:
            ok = ok & ~((ql[None, :] == 1) & (kl[:, None] == 0))
        if e == 6:
            ok = ok & ((ql[None, :] == 1) & (kl[:, None] == 0))
        drc = np.clip(dr, 0, 14)
        vals = rpb[:, drc, dc]
        out[:, :, e, :] = np.where(ok[None], vals, np.float32(-30000.0)).transpose(1, 0, 2)
    return out


def _pmaj(v, n):
    return np.ascontiguousarray(np.asarray(v, np.float32).reshape(n, 128).T)


def _wl(w, kc):
    w = np.asarray(w, np.float32)
    return np.ascontiguousarray(w.reshape(kc, 128, w.shape[1]).transpose(1, 0, 2))


def _host_layout(inp):
    L = 0
    sh = {}
    sh["w_in"] = _wl(inp["w_in"][L], 8)
    sh["w_kv"] = _wl(inp["w_mem_kv"][L], 8)
    sh["w_br"] = np.ascontiguousarray(np.stack([_wl(inp[k][L], 4) for k in ("w_br_ml", "w_br_na", "w_br_xa")]))
    sh["w_out"] = _wl(inp["w_out"][L], 8)
    sh["w_up"] = _wl(inp["w_ffn_up"][L], 8)
    sh["w_dn"] = _wl(inp["w_ffn_down"][L], 22)
    pv = np.zeros((128, NPV), np.float32)
    pv[:, 0:8] = _pmaj(inp["mix_norm_g"][L], 8)
    pv[:, 8:16] = _pmaj(inp["ffn_norm_g"][L], 8)
    pv[:, 16:24] = _pmaj(inp["mem_norm_g"][L], 8)
    pv[:, 24:48] = _pmaj(inp["b_merge_gate"][L], 24)
    wc = np.asarray(inp["w_ml_conv"][L], np.float32)
    pv[:, 48:72] = wc.reshape(3, 8, 128).transpose(2, 1, 0).reshape(128, 24)
    pv[:, 72:80] = _pmaj(inp["b_ml_conv"][L], 8)
    wf = np.asarray(inp["w_ffn_conv"][L], np.float32)
    pv[:, 80:146] = wf.reshape(3, 22, 128).transpose(2, 1, 0).reshape(128, 66)
    pv[:, 146:168] = _pmaj(inp["b_ffn_conv"][L], 22)
    pv[:, 168:172] = _pmaj(inp["ml_norm_g"][L], 4)
    sh["pvec"] = pv
    sh["fg"] = np.ascontiguousarray(np.broadcast_to(np.asarray(inp["final_norm_g"], np.float32)[None, :], (128, 1024)))
    big = np.asarray(inp["b_ml_igate"][L], np.float32); bfg = np.asarray(inp["b_ml_fgate"][L], np.float32)
    sh["gb"] = np.ascontiguousarray(np.concatenate([big[0], bfg[0], big[1], bfg[1]]).reshape(16, 1))
    sh["ident"] = np.eye(128, dtype=np.float32)
    tri = np.zeros((128, 2, 128), np.float32)
    s_ = np.arange(128)[:, None]; j_ = np.arange(128)[None, :]
    tri[:, 0, :] = (s_ <= j_); tri[:, 1, :] = (s_ >= j_)
    sh["tri"] = tri
    sel = np.zeros((16, 2, 8, 128), np.float32)
    for i in range(8):
        row = 4 + i if i < 4 else 8 + i
        sel[row, 0, i, :] = 1.0; sel[row, 1, i, :] = -1.0
    sh["sel"] = sel
    cmb = np.zeros((16, 3, 16), np.float32)
    for hd in range(4):
        cmb[hd, 0, hd] = 1.0; cmb[8 + hd, 0, 4 + hd] = 1.0
        cmb[4 + hd, 1, hd] = 1.0; cmb[12 + hd, 1, 4 + hd] = -1.0
        cmb[12 + hd, 2, 4 + hd] = 1.0
        cmb[4 + hd, 1, 8 + hd] = -1.0; cmb[12 + hd, 1, 8 + 4 + hd] = 1.0
        cmb[12 + hd, 2, 8 + 4 + hd] = -1.0
    sh["cmb"] = cmb
    sh["nab"] = _nab_table(np.asarray(inp["na_rpb"][L], np.float32))
    return sh


_NC_CACHE = {}


def kernel(**inputs):
    inp = {k: np.asarray(v) for k, v in inputs.items()}
    shared = _host_layout(inp)
    x = np.asarray(inp["x"], np.float32); mem = np.asarray(inp["mem"], np.float32)
    if "nc" not in _NC_CACHE:
        _NC_CACHE["nc"] = build(False)
    nc = _NC_CACHE["nc"]
    in_maps = []
    for b in range(8):
        m = dict(shared)
        m["x"] = np.ascontiguousarray(x[b]); m["mem"] = np.ascontiguousarray(mem[b])
        in_maps.append(m)
    res = run_bass_kernel_spmd(nc, in_maps, core_ids=list(range(8)))
    return np.stack([np.asarray(r["y"], np.float32).reshape(2048, 1024) for r in res.results], axis=0)
```
